# Optimizing a Trainium2 kernel written in Bass

```python
import math
import jax
import jax.numpy as jnp
from jax import lax
import numpy as np

D_MODEL = 1024
BATCH = 16
SEQ = 2048
DEPTH = 2

GRID_W = 64
CTX_LEN = 256
N_HEADS = 8
HEAD_DIM = 64
V_HEAD_DIM = 2 * HEAD_DIM
ATTN_WIDTH = N_HEADS * V_HEAD_DIM
N_FOURIER_GROUPS = 4
FOURIER_GROUP_DIM = 128
FOURIER_WIDTH = N_FOURIER_GROUPS * FOURIER_GROUP_DIM
N_BRANCHES = 2
KV_COLS = 2 * ATTN_WIDTH
IN_WIDTH = 3 * ATTN_WIDTH + FOURIER_WIDTH + N_BRANCHES * D_MODEL
D_FF = -(-8 * D_MODEL // (3 * 256)) * 256
N_MOD = 6
ROPE_THETA = 10000.0
Q_BLOCK = 128
EPS = 1e-6

kernel_name = "hybrid_diffattn_fourier_dit"


def rmsnorm(x, g):
    xf = x.astype(jnp.float32)
    y = xf * lax.rsqrt(jnp.mean(xf * xf, axis=-1, keepdims=True) + EPS)
    return (y * g.astype(jnp.float32)).astype(x.dtype)


def modulate(h, shift, scale):
    return h * (1 + scale) + shift


def axial_rope_tables(n_tokens):
    rows = n_tokens // GRID_W
    r, col = jnp.meshgrid(jnp.arange(rows), jnp.arange(GRID_W), indexing="ij")
    r = r.reshape(-1).astype(jnp.float32)
    col = col.reshape(-1).astype(jnp.float32)
    axis_dim = HEAD_DIM // 2
    inv_freq = ROPE_THETA ** (-jnp.arange(0, axis_dim, 2, dtype=jnp.float32) / axis_dim)
    ang_r = r[:, None] * inv_freq[None, :]
    ang_c = col[:, None] * inv_freq[None, :]
    ang = jnp.concatenate([ang_r, ang_r, ang_c, ang_c], axis=-1)
    return jnp.cos(ang), jnp.sin(ang)


def rotate_half_axial(x):
    shp = x.shape
    xs = x.reshape(shp[:-1] + (2, 2, HEAD_DIM // 4))
    return jnp.stack([-xs[..., 1, :], xs[..., 0, :]], axis=-2).reshape(shp)


def apply_rope(x, cos, sin):
    cos = cos[None, :, None, None, :].astype(x.dtype)
    sin = sin[None, :, None, None, :].astype(x.dtype)
    return x * cos + rotate_half_axial(x) * sin


def diff_attention(q, k, v, lam):
    s = jnp.einsum("bqhid,bkhid->bhiqk", q.astype(jnp.float32), k.astype(jnp.float32)) * (HEAD_DIM ** -0.5)
    p = jax.nn.softmax(s, axis=-1)
    a = p[:, :, 0] - lam * p[:, :, 1]
    return jnp.einsum("bhqk,bkhe->bqhe", a, v.astype(jnp.float32)).astype(v.dtype)


def fourier_mix(f):
    b, t, _ = f.shape
    fg = f.reshape(b, t, N_FOURIER_GROUPS, FOURIER_GROUP_DIM).astype(jnp.float32)
    out = jnp.fft.fft2(fg, axes=(1, 3), norm="ortho").real
    return out.reshape(b, t, FOURIER_WIDTH).astype(f.dtype)


def swiglu(h, w_gate_up, w_down):
    gu = h @ w_gate_up
    return (jax.nn.silu(gu[..., :D_FF]) * gu[..., D_FF:]) @ w_down


def trunk_layer(x, cx, mod_lat, mod_ctx, cos, sin, lp, lam_init, ctx_out):
    b, t, _ = x.shape
    tc = cx.shape[1]
    sh1, sc1, gt1, sh2, sc2, gt2 = jnp.split(mod_lat, N_MOD, axis=-1)
    csh1, csc1, cgt1, csh2, csc2, cgt2 = jnp.split(mod_ctx, N_MOD, axis=-1)
    f32 = jnp.float32
    lam = (jnp.exp(jnp.sum(lp["lambda_q1"].astype(f32) * lp["lambda_k1"].astype(f32)))
           - jnp.exp(jnp.sum(lp["lambda_q2"].astype(f32) * lp["lambda_k2"].astype(f32))) + lam_init)

    def kv_heads(p, n):
        k = rmsnorm(p[..., :ATTN_WIDTH].reshape(b, n, N_HEADS, 2, HEAD_DIM), lp["k_norm_g"])
        v = p[..., ATTN_WIDTH:KV_COLS].reshape(b, n, N_HEADS, V_HEAD_DIM)
        return k, v

    def q_heads(p, n):
        return rmsnorm(p[..., KV_COLS:KV_COLS + ATTN_WIDTH].reshape(b, n, N_HEADS, 2, HEAD_DIM), lp["q_norm_g"])

    def merged_branches(attn_o, p_rest):
        n = attn_o.shape[1]
        o = rmsnorm(attn_o, lp["subln_g"]) * (1 - lam_init)
        y_a = o.reshape(b, n, ATTN_WIDTH) @ lp["w_proj_attn"]
        y_f = fourier_mix(p_rest[..., :FOURIER_WIDTH]) @ lp["w_proj_fourier"]
        g_a = jax.nn.sigmoid(p_rest[..., FOURIER_WIDTH:FOURIER_WIDTH + D_MODEL])
        g_f = jax.nn.sigmoid(p_rest[..., FOURIER_WIDTH + D_MODEL:])
        return (g_a * y_a + g_f * y_f) @ lp["w_out"]

    w_in = lp["w_in"]
    h = modulate(rmsnorm(x, lp["norm1_g"]), sh1, sc1)
    hc = modulate(rmsnorm(cx, lp["norm1_g"]), csh1, csc1)
    p = h @ w_in
    pc = hc @ (w_in if ctx_out else w_in[:, :KV_COLS])

    k, v = kv_heads(p, t)
    kc, vc = kv_heads(pc, tc)
    q = apply_rope(q_heads(p, t), cos, sin)
    k = apply_rope(k, cos, sin)
    k_all = jnp.concatenate([kc, k], axis=1)
    v_all = jnp.concatenate([vc, v], axis=1)
    n_blocks = t // Q_BLOCK
    qb = q.reshape(b, n_blocks, Q_BLOCK, N_HEADS, 2, HEAD_DIM).swapaxes(0, 1)
    ob = lax.map(lambda qi: diff_attention(qi, k_all, v_all, lam), qb)
    o = ob.swapaxes(0, 1).reshape(b, t, N_HEADS, V_HEAD_DIM)
    x = x + gt1 * merged_branches(o, p[..., KV_COLS + ATTN_WIDTH:])

    h2 = modulate(rmsnorm(x, lp["norm2_g"]), sh2, sc2)
    x = x + gt2 * swiglu(h2, lp["w_gate_up"], lp["w_down"])

    if not ctx_out:
        return x, cx

    oc = diff_attention(q_heads(pc, tc), kc, vc, lam)
    cx = cx + cgt1 * merged_branches(oc, pc[..., KV_COLS + ATTN_WIDTH:])
    hc2 = modulate(rmsnorm(cx, lp["norm2_g"]), csh2, csc2)
    cx = cx + cgt2 * swiglu(hc2, lp["w_gate_up"], lp["w_down"])
    return x, cx


def setup_inputs(seed: int = 0) -> dict:
    key = jax.random.key(seed)
    ks = jax.random.split(key, 21)

    def nrm(k, shape, scale):
        return jax.random.normal(k, shape, jnp.float32) * scale

    def gain(k, shape):
        return 1.0 + 0.02 * jax.random.normal(k, shape, jnp.float32)

    return {
        "x": nrm(ks[0], (BATCH, SEQ, D_MODEL), 1.0),
        "c": nrm(ks[1], (BATCH, D_MODEL), 1.0),
        "ctx": nrm(ks[2], (BATCH, CTX_LEN, D_MODEL), 1.0),
        "c_ctx": nrm(ks[3], (D_MODEL,), 1.0),
        "w_ada": nrm(ks[4], (DEPTH, D_MODEL, N_MOD * D_MODEL), D_MODEL ** -0.5),
        "b_ada": nrm(ks[5], (DEPTH, N_MOD * D_MODEL), 0.01),
        "norm1_g": gain(ks[6], (DEPTH, D_MODEL)),
        "norm2_g": gain(ks[7], (DEPTH, D_MODEL)),
        "w_in": nrm(ks[8], (DEPTH, D_MODEL, IN_WIDTH), D_MODEL ** -0.5),
        "q_norm_g": gain(ks[9], (DEPTH, HEAD_DIM)),
        "k_norm_g": gain(ks[10], (DEPTH, HEAD_DIM)),
        "lambda_q1": nrm(ks[11], (DEPTH, HEAD_DIM), 0.1),
        "lambda_k1": nrm(ks[12], (DEPTH, HEAD_DIM), 0.1),
        "lambda_q2": nrm(ks[13], (DEPTH, HEAD_DIM), 0.1),
        "lambda_k2": nrm(ks[14], (DEPTH, HEAD_DIM), 0.1),
        "subln_g": gain(ks[15], (DEPTH, V_HEAD_DIM)),
        "w_proj_attn": nrm(ks[16], (DEPTH, ATTN_WIDTH, D_MODEL), ATTN_WIDTH ** -0.5),
        "w_proj_fourier": nrm(ks[17], (DEPTH, FOURIER_WIDTH, D_MODEL), FOURIER_WIDTH ** -0.5),
        "w_out": nrm(ks[18], (DEPTH, D_MODEL, D_MODEL), D_MODEL ** -0.5),
        "w_gate_up": nrm(ks[19], (DEPTH, D_MODEL, 2 * D_FF), D_MODEL ** -0.5),
        "w_down": nrm(ks[20], (DEPTH, D_FF, D_MODEL), D_FF ** -0.5),
    }


def reference(x, c, ctx, c_ctx, w_ada, b_ada, norm1_g, norm2_g, w_in, q_norm_g, k_norm_g,
              lambda_q1, lambda_k1, lambda_q2, lambda_k2, subln_g, w_proj_attn, w_proj_fourier,
              w_out, w_gate_up, w_down):
    cos, sin = axial_rope_tables(x.shape[1])
    silu_c = jax.nn.silu(c)
    silu_cc = jax.nn.silu(c_ctx)
    cx = ctx
    for l in range(DEPTH):
        lp = {
            "norm1_g": norm1_g[l], "norm2_g": norm2_g[l], "w_in": w_in[l],
            "q_norm_g": q_norm_g[l], "k_norm_g": k_norm_g[l],
            "lambda_q1": lambda_q1[l], "lambda_k1": lambda_k1[l],
            "lambda_q2": lambda_q2[l], "lambda_k2": lambda_k2[l],
            "subln_g": subln_g[l], "w_proj_attn": w_proj_attn[l],
            "w_proj_fourier": w_proj_fourier[l], "w_out": w_out[l],
            "w_gate_up": w_gate_up[l], "w_down": w_down[l],
        }
        mod_lat = (silu_c @ w_ada[l] + b_ada[l])[:, None, :]
        mod_ctx = (silu_cc @ w_ada[l] + b_ada[l])[None, None, :]
        lam_init = 0.8 - 0.6 * math.exp(-0.3 * l)
        x, cx = trunk_layer(x, cx, mod_lat, mod_ctx, cos, sin, lp, lam_init, l < DEPTH - 1)
    return x
```

```python
import math
import numpy as np
import ml_dtypes
import concourse.bass as bass
import concourse.mybir as mybir
from concourse.bass_utils import run_bass_kernel_spmd

F32 = mybir.dt.float32
BF16 = mybir.dt.bfloat16
AF = mybir.ActivationFunctionType
ALU = mybir.AluOpType
AX = mybir.AxisListType

D = 1024
T = 2048
TC = 256
TA = T + TC
NH = 8
DFF = 2816
NJ = DFF // 128
INW = 5632
EPS = 1e-6
DEPTH = 2
NB = 2
LAM_INIT = [0.8 - 0.6 * math.exp(-0.3 * l) for l in range(DEPTH)]
TILES = [(0, 512, 0), (512, 512, 0), (1024, 512, 0), (1536, 512, 0), (2048, 256, 1)]

SAME_SYNC = True
SB0 = 16512
SB_END = 229344


class Sched:
    EPOCH = 4000
    R = 8
    ENGS = ("pe", "act", "dve", "pool")

    def __init__(self, nc, rank=None):
        self.nc = nc
        self.rank = rank
        self.h = {"pe": nc.tensor, "act": nc.scalar, "dve": nc.vector, "pool": nc.gpsimd, "sp": nc.sync}
        self.idx = {e: 0 for e in self.ENGS}
        self.needed = {e: set() for e in self.ENGS}
        self.waited = {}
        self.state = {}
        self.released = {}
        self.dman = {"sp": 0, "pool": 0, "act": 0}
        self.dcount = {}
        self.esems = {e: [] for e in self.ENGS}
        self.dsems = {}
        self.same_sync = SAME_SYNC
        if rank is not None:
            for e in self.ENGS:
                n = len(rank[e])
                for k in range((n + self.EPOCH - 1) // self.EPOCH + 1):
                    self.esems[e].append(nc.alloc_semaphore("s_%s_%d" % (e, k)))
            for q in ("sp", "pool", "act"):
                for i in range(self.R):
                    self.dsems[(q, i)] = nc.alloc_semaphore("d_%s_%d" % (q, i))

    def _st(self, k):
        st = self.state.get(k)
        if st is None:
            st = [None, dict(self.released)]
            self.state[k] = st
        return st

    def free(self, *names):
        for k in list(self.state.keys()):
            if k[0] in names:
                st = self.state.pop(k)
                if st[0] is not None:
                    p, v = st[0]
                    if self.released.get(p, 0) < v:
                        self.released[p] = v
                for p, v in st[1].items():
                    if self.released.get(p, 0) < v:
                        self.released[p] = v

    def _deps(self, reads, writes):
        deps = {}
        for k in reads:
            st = self._st(k)
            if st[0] is not None:
                p, v = st[0]
                if deps.get(p, 0) < v:
                    deps[p] = v
        for k in writes:
            st = self._st(k)
            if st[0] is not None:
                p, v = st[0]
                if deps.get(p, 0) < v:
                    deps[p] = v
            for p, v in st[1].items():
                if deps.get(p, 0) < v:
                    deps[p] = v
        return deps

    def _wait(self, eng, deps):
        for p, v in deps.items():
            if p == eng and (eng == "pe" or not self.same_sync):
                continue
            key = (eng, p)
            if self.waited.get(key, 0) >= v:
                continue
            self.waited[key] = v
            if isinstance(p, str):
                self.needed[p].add(v)
                if self.rank is not None:
                    r = self.rank[p][v]
                    self.h[eng].wait_ge(self.esems[p][(r - 1) // self.EPOCH], (r - 1) % self.EPOCH + 1)
            else:
                if self.rank is not None:
                    self.h[eng].wait_ge(self.dsems[p], v)

    def _record(self, ev, reads, writes):
        for k in writes:
            st = self._st(k)
            st[0] = ev
            st[1] = {}
        p, v = ev
        for k in reads:
            st = self._st(k)
            if st[1].get(p, 0) < v:
                st[1][p] = v

    def op(self, eng, fn, reads=(), writes=()):
        self._wait(eng, self._deps(reads, writes))
        i = self.idx[eng] + 1
        self.idx[eng] = i
        inst = fn()
        if self.rank is not None:
            r = self.rank[eng].get(i)
            if r is not None:
                inst.then_inc(self.esems[eng][(r - 1) // self.EPOCH], 1)
        self._record((eng, i), reads, writes)

    def dma(self, q, fn, reads=(), writes=()):
        deps = self._deps(reads, writes)
        n = self.dman[q]
        self.dman[q] = n + 1
        prod = (q, n % self.R)
        prev = self.dcount.get(prod, 0)
        if prev > 0 and deps.get(prod, 0) < prev:
            deps[prod] = prev
        self._wait(q, deps)
        cnt = prev + 16
        self.dcount[prod] = cnt
        inst = fn()
        if self.rank is not None:
            inst.then_inc(self.dsems[prod], 16)
        self._record((prod, cnt), reads, writes)

    def finish(self):
        if self.rank is None:
            for e in self.ENGS:
                if self.idx[e] > 0:
                    self.needed[e].add(self.idx[e])
            return
        for prod, cnt in self.dcount.items():
            self.nc.sync.wait_ge(self.dsems[prod], cnt)
        for e in self.ENGS:
            if self.idx[e] > 0:
                r = self.rank[e][self.idx[e]]
                self.nc.sync.wait_ge(self.esems[e][(r - 1) // self.EPOCH], (r - 1) % self.EPOCH + 1)


def build_program(rank=None, nb=NB, depth=DEPTH, dbg=None):
    nc = bass.Bass("TRN2", target_bir_lowering=False)
    S = Sched(nc, rank)

    def din(name, shape, dt=F32):
        return nc.dram_tensor(name, list(shape), dt, kind="ExternalInput").ap()

    x_d = din("x2", [NB, T, D])
    ctx_d = din("ctx2", [NB, TC, D])
    cT_d = din("cT", [128, 8, 4])
    wada_d = din("w_ada", [DEPTH, D, 6 * D])
    bada_d = din("bada", [128, DEPTH, 48])
    n1g_d = din("n1g", [128, DEPTH, 8])
    n2g_d = din("n2g", [128, DEPTH, 8])
    win_d = din("w_in", [DEPTH, D, INW])
    qg_d = din("qg", [128, DEPTH])
    kg_d = din("kg", [128, DEPTH])
    sg_d = din("sublng", [128, DEPTH])
    lamv_d = din("lamv", [128, DEPTH, 4, 64])
    wpa_d = din("w_proj_attn", [DEPTH, D, D])
    wpf_d = din("w_proj_fourier", [DEPTH, 512, D])
    wo_d = din("w_out", [DEPTH, D, D])
    wgu_d = din("w_gate_up", [DEPTH, D, 2 * DFF])
    wd_d = din("w_down", [DEPTH, DFF, D])
    cb_d = din("cbf", [128, 5, 128], BF16)
    ident_d = din("ident", [128, 128])
    rope_d = din("rope", [128, 2, T], BF16)
    c256_d = din("c256", [128, 2, 2, 256], BF16)
    ctab_d = din("ctab", [4, 2, 128, 8 * 512], BF16)
    stab_d = din("stab", [4, 2, 128, 8 * 512], BF16)
    out_d = nc.dram_tensor("out", [NB, T, D], F32, kind="ExternalOutput").ap()
    xs = nc.dram_tensor("xs", [5, 128, 8 * 512], F32, kind="Internal").ap()
    gates = nc.dram_tensor("gates", [5, 128, 16 * 512], BF16, kind="Internal").ap()
    fos = nc.dram_tensor("fos", [5, 128, 4 * 512], BF16, kind="Internal").ap()
    dbg_d = {}
    if dbg:
        for name, shape, dt in dbg:
            dbg_d[name] = nc.dram_tensor("dbg_" + name, list(shape), dt, kind="ExternalOutput").ap()

    cur = [SB0]

    def salloc(name, shape, dt, at=None):
        nbytes = int(np.prod(shape[1:])) * (4 if dt == F32 else 2)
        if at is None:
            off = cur[0]
            cur[0] = (off + nbytes + 31) // 32 * 32
        else:
            off = at
        assert off % 32 == 0 and off + nbytes <= SB_END, (name, off, nbytes)
        return nc.alloc_sbuf_tensor_at(name, list(shape), dt, offset=off)

    identF = salloc("identF", [128, 128], F32)
    cb = salloc("cb", [128, 5, 128], BF16)
    rope = salloc("rope", [128, 2, T], BF16)
    c256 = salloc("c256", [128, 2, 2, 256], BF16)
    cT = salloc("cT", [128, 8, 4], F32)
    scT = salloc("scT", [128, 8, 4], F32)
    bada = salloc("bada", [128, DEPTH, 48], F32)
    n1g = salloc("n1g", [128, DEPTH, 8], F32)
    n2g = salloc("n2g", [128, DEPTH, 8], F32)
    qg = salloc("qg", [128, DEPTH], F32)
    kg = salloc("kg", [128, DEPTH], F32)
    sg = salloc("sg", [128, DEPTH], F32)
    lamv = salloc("lamv", [128, DEPTH, 4, 64], F32)
    modT = salloc("modT", [128, DEPTH, 48, 4], F32)
    Gm = salloc("Gm", [128, DEPTH, 3, 2, 8], F32)
    neglam = salloc("neglam", [128, DEPTH], F32)
    gsub = salloc("gsub", [128, DEPTH], F32)
    epsc = salloc("epsc", [128, 1], F32)
    lamt = salloc("lamt", [128, 8], F32)
    lamp = salloc("lamp", [128, 64], F32)
    A0 = cur[0]
    ARENA = SB_END - A0
    assert ARENA >= 190000, ARENA
    R0 = A0
    R1 = A0 + 36864
    R2 = A0 + 73728
    R3 = A0 + 110592
    R4 = A0 + 147456

    onesB = cb[:, 0, :]
    bdB = cb[:, 1, :]
    rotB = cb[:, 2, :]
    CcB = cb[:, 3, :]
    ScnB = cb[:, 4, :]
    cosT = rope[:, 0, :]
    sinT = rope[:, 1, :]

    ps = nc.alloc_psum_tensor("ps", [128, 4096], F32)

    def bank(i, w=512):
        return ps[:, i * 512:i * 512 + w]

    def bk(i):
        return ("ps", i)

    V = nc.vector
    A = nc.scalar
    G = nc.gpsimd
    PE = nc.tensor

    cnt = {"b": 0, "cp": 0}

    def nextbank(lo, hi):
        n = hi - lo
        cnt["b"] += 1
        return lo + cnt["b"] % n

    def copy_any(out, in_, reads, writes):
        cnt["cp"] += 1
        if cnt["cp"] % 2:
            S.op("act", lambda: A.activation(out=out, in_=in_, func=AF.Identity), reads, writes)
        else:
            S.op("dve", lambda: V.tensor_copy(out=out, in_=in_), reads, writes)

    def ld(q, out, in_, wkey, rkeys=()):
        S.dma(q, lambda: (nc.sync if q == "sp" else nc.gpsimd).dma_start(out=out, in_=in_), rkeys, [wkey])

    ld("sp", identF[:], ident_d, ("identF",))
    ld("sp", cb[:], cb_d, ("cb",))
    ld("sp", rope[:], rope_d, ("rope",))
    ld("sp", c256[:], c256_d, ("c256",))
    ld("sp", cT[:], cT_d, ("cT",))
    ld("sp", bada[:], bada_d, ("bada",))
    ld("sp", n1g[:], n1g_d, ("n1g",))
    ld("sp", n2g[:], n2g_d, ("n2g",))
    ld("sp", qg[:], qg_d, ("qg",))
    ld("sp", kg[:], kg_d, ("kg",))
    ld("sp", sg[:], sg_d, ("sg",))
    ld("sp", lamv[:], lamv_d, ("lamv",))
    S.op("dve", lambda: V.memset(epsc[:], EPS), [], [("epsc",)])
    S.op("act", lambda: A.activation(out=scT[:], in_=cT[:], func=AF.Silu), [("cT",)], [("scT",)])

    wa_bufs = [salloc("wa%d" % i, [128, 8, 512], BF16, at=R0 + i * 8192) for i in range(3)]
    scTb = salloc("scTb", [128, 8, 4], BF16, at=R0 + 3 * 8192)
    S.op("dve", lambda: V.tensor_copy(out=scTb[:], in_=scT[:]), [("scT",)], [("scTb",)])
    wada_v = [wada_d[l].rearrange("(kc p) c -> p kc c", p=128) for l in range(DEPTH)]
    gi = 0
    for l in range(depth):
        for jg in range(12):
            wa = wa_bufs[gi % 3]
            wk = ("wa", gi % 3)
            gi += 1
            S.dma("pool", lambda wa=wa, l=l, jg=jg: nc.gpsimd.dma_start(out=wa[:], in_=wada_v[l][:, :, jg * 512:(jg + 1) * 512]), [], [wk])
            for jb in range(4):
                j = jg * 4 + jb
                bi = nextbank(0, 8)
                for kc in range(8):
                    S.op("pe", lambda wa=wa, jb=jb, kc=kc, bi=bi: PE.matmul(bank(bi, 4), lhsT=wa[:, kc, jb * 128:(jb + 1) * 128], rhs=scTb[:, kc, :], start=(kc == 0), stop=(kc == 7)),
                         [wk, ("scTb",)], [bk(bi)])
                S.op("dve", lambda l=l, j=j, bi=bi: V.tensor_scalar(out=modT[:, l, j, :], in0=bank(bi, 4), scalar1=bada[:, l, j:j + 1], scalar2=None, op0=ALU.add),
                     [bk(bi), ("bada",)], [("modT",)])
        for who in range(3):
            S.op("dve", lambda l=l, who=who: V.scalar_tensor_tensor(out=Gm[:, l, who, 0, :], in0=modT[:, l, 8:16, who], scalar=1.0, in1=n1g[:, l, :], op0=ALU.add, op1=ALU.mult),
                 [("modT",), ("n1g",)], [("Gm",)])
            S.op("dve", lambda l=l, who=who: V.scalar_tensor_tensor(out=Gm[:, l, who, 1, :], in0=modT[:, l, 32:40, who], scalar=1.0, in1=n2g[:, l, :], op0=ALU.add, op1=ALU.mult),
                 [("modT",), ("n2g",)], [("Gm",)])
        for i in range(2):
            S.op("dve", lambda l=l, i=i: V.tensor_tensor(out=lamp[:], in0=lamv[:, l, 2 * i, :], in1=lamv[:, l, 2 * i + 1, :], op=ALU.mult), [("lamv",)], [("lamp",)])
            S.op("dve", lambda i=i: V.reduce_sum(out=lamt[:, i:i + 1], in_=lamp[:], axis=AX.X), [("lamp",)], [("lamt",)])
        S.op("act", lambda: A.activation(out=lamt[:, 2:4], in_=lamt[:, 0:2], func=AF.Exp), [("lamt",)], [("lamt",)])
        S.op("dve", lambda l=l: V.tensor_tensor(out=lamt[:, 4:5], in0=lamt[:, 3:4], in1=lamt[:, 2:3], op=ALU.subtract), [("lamt",)], [("lamt",)])
        S.op("dve", lambda l=l: V.tensor_scalar(out=neglam[:, l:l + 1], in0=lamt[:, 4:5], scalar1=-LAM_INIT[l], scalar2=None, op0=ALU.add), [("lamt",)], [("neglam",)])
        S.op("dve", lambda l=l: V.tensor_scalar(out=gsub[:, l:l + 1], in0=sg[:, l:l + 1], scalar1=(1.0 - LAM_INIT[l]), scalar2=None, op0=ALU.mult), [("sg",)], [("gsub",)])
    S.free("wa", "scTb")

    def mcol(l, part, c, who):
        return modT[:, l, part * 8 + c, who:who + 1]

    def xs_cols(c0, w):
        return xs[c0 // 512].rearrange("p (c t) -> p c t", c=8)[:, :, :w]

    def xs_chunk(mc, ti, w):
        return xs[ti][:, mc * 512:mc * 512 + w]

    def norm_phase(l, b, which, tiles, hT):
        xt_b = [salloc("nx%d" % i, [128, 8, 512], F32, at=R1 + i * 16384) for i in range(3)]
        sq_b = [salloc("nsq%d" % i, [128, 8, 512], BF16, at=R1 + 49152 + i * 8192) for i in range(2)]
        ln_b = [salloc("nln%d" % i, [128, 512], F32, at=R1 + 65536 + i * 2048) for i in range(2)]
        rs_b = [salloc("nrs%d" % i, [128, 512], F32, at=R1 + 69632 + i * 2048) for i in range(2)]
        for ti, (c0, W, isc) in enumerate(tiles):
            who = 2 if isc else b
            xt = xt_b[ti % 3]
            sq = sq_b[ti % 2]
            ln = ln_b[ti % 2]
            rs = rs_b[ti % 2]
            kx, ksq, kln, krs = ("nx", ti % 3), ("nsq", ti % 2), ("nln", ti % 2), ("nrs", ti % 2)
            S.dma("sp", lambda xt=xt, c0=c0, W=W: nc.sync.dma_start(out=xt[:, :, :W], in_=xs_cols(c0, W)), [("xs", ti)], [kx])
            S.op("pool", lambda xt=xt, sq=sq, W=W: G.tensor_tensor(out=sq[:, 0:4, :W], in0=xt[:, 0:4, :W], in1=xt[:, 0:4, :W], op=ALU.mult), [kx], [(ksq[0], ksq[1], 0)])
            S.op("dve", lambda xt=xt, sq=sq, W=W: V.tensor_tensor(out=sq[:, 4:8, :W], in0=xt[:, 4:8, :W], in1=xt[:, 4:8, :W], op=ALU.mult), [kx], [(ksq[0], ksq[1], 1)])
            bi = nextbank(0, 8)
            for c in range(8):
                S.op("pe", lambda sq=sq, c=c, bi=bi, W=W: PE.matmul(bank(bi, W), lhsT=onesB, rhs=sq[:, c, :W], start=(c == 0), stop=(c == 7)), [(ksq[0], ksq[1], c // 4), ("cb",)], [bk(bi)])
            S.op("act", lambda ln=ln, bi=bi, W=W: A.activation(out=ln[:, :W], in_=bank(bi, W), func=AF.Ln, scale=1.0 / D, bias=epsc[:]), [bk(bi), ("epsc",)], [kln])
            S.op("act", lambda ln=ln, rs=rs, W=W: A.activation(out=rs[:, :W], in_=ln[:, :W], func=AF.Exp, scale=-0.5), [kln], [krs])
            S.op("dve", lambda xt=xt, rs=rs, W=W: V.tensor_tensor(out=xt[:, :, :W], in0=xt[:, :, :W], in1=rs[:, :W].unsqueeze(1).broadcast_to([128, 8, W]), op=ALU.mult), [kx, krs], [kx])
            for c in range(8):
                S.op("act", lambda xt=xt, c=c, c0=c0, W=W, who=who: A.activation(out=hT[:, c, c0:c0 + W], in_=xt[:, c, :W], func=AF.Identity,
                                                                                 scale=Gm[:, l, who, which, c:c + 1], bias=mcol(l, 3 * which, c, who)),
                     [kx, ("Gm",), ("modT",)], [("hT", c, ti)])
        S.free("nx", "nsq", "nln", "nrs")

    for b in range(nb):
        xin_b = [salloc("xin%d" % i, [128, 4, D], F32, at=R0 + i * 16384) for i in range(2)]
        xtt_b = [salloc("xtt%d" % i, [128, 8, 512], F32, at=R0 + 32768 + i * 16384) for i in range(2)]

        def p0_load(g):
            c0, W, isc = TILES[g]
            nt = W // 128
            xin = xin_b[g % 2]
            src = (ctx_d[b] if isc else x_d[b, c0:c0 + W, :]).rearrange("(t p) d -> p t d", p=128)
            S.dma("sp", lambda: nc.sync.dma_start(out=xin[:, :nt, :], in_=src), [], [("xin", g % 2)])

        p0_load(0)
        p0_load(1)
        for g, (c0, W, isc) in enumerate(TILES):
            nt = W // 128
            xin = xin_b[g % 2]
            xtt = xtt_b[g % 2]
            for t in range(nt):
                pb = 2 * t
                for j in range(8):
                    S.op("pe", lambda t=t, j=j, pb=pb: PE.transpose(out=ps[:, pb * 512 + j * 128:pb * 512 + (j + 1) * 128], in_=xin[:, t, j * 128:(j + 1) * 128], identity=identF[:]),
                         [("xin", g % 2), ("identF",)], [bk(pb + j // 4)])
                copy_any(xtt[:, :, t * 128:(t + 1) * 128], ps[:, pb * 512:pb * 512 + 1024].rearrange("p (c t) -> p c t", c=8), [bk(pb), bk(pb + 1)], [("xtt", g % 2, t)])
            if g + 2 < len(TILES):
                p0_load(g + 2)
            S.dma("sp", lambda: nc.sync.dma_start(out=xs_cols(c0, W), in_=xtt[:, :, :W]), [("xtt", g % 2, t) for t in range(nt)], [("xs", g)])
        S.free("xin", "xtt")

        for l in range(depth):
            ctx_out = l < DEPTH - 1
            all_tiles = TILES
            lat_tiles = TILES[:4]
            out_tiles = TILES if ctx_out else lat_tiles

            wbufs = [salloc("wb%d" % i, [128, 8, 512], BF16, at=R4 + i * 8192) for i in range(2)]
            win_v = win_d[l].rearrange("(kc p) c -> p kc c", p=128)
            wsched = [3072, 0, 512, 1024, 1536, 2048, 2560, 3584, 4096, 4608, 5120]
            wst = {"issued": 0, "used": 0}

            def issue_wload():
                g_ = wst["issued"]
                if g_ >= len(wsched):
                    return
                wst["issued"] += 1
                i = g_ % 2
                wb = wbufs[i]
                col0 = wsched[g_]
                S.dma("pool", lambda: nc.gpsimd.dma_start(out=wb[:], in_=win_v[:, :, col0:col0 + 512]), [], [("wb", i)])

            def load_wgroup(col0):
                g_ = wst["used"]
                assert wsched[g_] == col0
                wst["used"] += 1
                while wst["issued"] <= g_ + 1:
                    if wst["issued"] >= len(wsched):
                        break
                    issue_wload()
                i = g_ % 2
                return wbufs[i], ("wb", i)

            issue_wload()
            hT = salloc("hT", [128, 8, TA], BF16, at=R0)
            norm_phase(l, b, 0, all_tiles, hT)

            def proj_block(wb, wk, cbk, c0, W, ti, lo=0, hi=4):
                bi = nextbank(lo, hi)
                for kc in range(8):
                    S.op("pe", lambda kc=kc: PE.matmul(bank(bi, W), lhsT=wb[:, kc, cbk * 128:(cbk + 1) * 128], rhs=hT[:, kc, c0:c0 + W], start=(kc == 0), stop=(kc == 7)),
                         [wk, ("hT", kc, ti)], [bk(bi)])
                return bi

            fT = salloc("fT", [128, 4, TA], BF16, at=R3)
            wb, wk = load_wgroup(3072)
            for cbk in range(4):
                for ti, (c0, W, isc) in enumerate(out_tiles):
                    bi = proj_block(wb, wk, cbk, c0, W, ti)
                    copy_any(fT[:, cbk, c0:c0 + W], bank(bi, W), [bk(bi)], [("fT", cbk, ti)])

            AB = salloc("AB", [128, 18, 1024], BF16, at=R1)
            tabs = [[salloc("tab%d%d" % (s_, hf), [128, 8, 512], BF16, at=R2 + (s_ * 2 + hf) * 8192) for hf in range(2)] for s_ in range(2)]
            fost = [salloc("fost%d" % i, [128, 4, 512], BF16, at=R3 + 18432 + i * 4096) for i in range(2)]
            tab_v = [ctab_d, stab_d]

            def load_tabs(tq, hf):
                for s_ in range(2):
                    S.dma("sp", lambda s_=s_: nc.sync.dma_start(out=tabs[s_][hf][:].rearrange("p t q -> p (t q)"), in_=tab_v[s_][tq, hf]), [], [("tab", s_, hf)])

            load_tabs(0, 0)
            load_tabs(0, 1)
            ntt = 18 if ctx_out else 16
            for t in range(ntt):
                ti = t // 4 if t < 16 else 4
                ba = 2 * (t % 4)
                for g in range(4):
                    S.op("pe", lambda t=t, g=g, ba=ba: PE.matmul(ps[:, ba * 512 + g * 128:ba * 512 + (g + 1) * 128], lhsT=fT[:, g, t * 128:(t + 1) * 128], rhs=CcB, start=True, stop=True),
                         [("fT", g, ti), ("cb",)], [bk(ba)])
                for g in range(4):
                    S.op("pe", lambda t=t, g=g, ba=ba: PE.matmul(ps[:, (ba + 1) * 512 + g * 128:(ba + 1) * 512 + (g + 1) * 128], lhsT=fT[:, g, t * 128:(t + 1) * 128], rhs=ScnB, start=True, stop=True),
                         [("fT", g, ti), ("cb",)], [bk(ba + 1)])
                copy_any(AB[:, t, :], ps[:, ba * 512:ba * 512 + 1024], [bk(ba), bk(ba + 1)], [("AB", t)])
            for tq in range(4):
                c0 = tq * 512
                ab = (tq % 2) * 4
                for hf in range(2):
                    for j in range(4):
                        for tl in range(8):
                            t = hf * 8 + tl
                            for s_ in range(2):
                                S.op("pe", lambda j=j, t=t, tl=tl, s_=s_, hf=hf, ab=ab: PE.matmul(bank(ab + j), lhsT=AB[:, t, s_ * 512 + j * 128:s_ * 512 + (j + 1) * 128], rhs=tabs[s_][hf][:, tl, :],
                                                                                          start=(t == 0 and s_ == 0), stop=(t == 15 and s_ == 1)),
                                     [("AB", t), ("tab", s_, hf)], [bk(ab + j)])
                    if tq + 1 < 4:
                        load_tabs(tq + 1, hf)
                fo = fost[tq % 2]
                for j in range(4):
                    copy_any(fo[:, j, :], bank(ab + j), [bk(ab + j)], [("fost", tq % 2)])
                S.dma("sp", lambda fo=fo, c0=c0: nc.sync.dma_start(out=fos[tq].rearrange("p (j t) -> p j t", j=4), in_=fo[:]), [("fost", tq % 2)], [("fos", tq)])
            if ctx_out:
                fo = fost[0]
                for j in range(4):
                    bi = nextbank(0, 8)
                    for tl in range(2):
                        for s_ in range(2):
                            S.op("pe", lambda j=j, tl=tl, s_=s_, bi=bi: PE.matmul(bank(bi, 256), lhsT=AB[:, 16 + tl, s_ * 512 + j * 128:s_ * 512 + (j + 1) * 128], rhs=c256[:, s_, tl, :],
                                                                                  start=(tl == 0 and s_ == 0), stop=(tl == 1 and s_ == 1)),
                                 [("AB", 16 + tl), ("c256",)], [bk(bi)])
                    copy_any(fo[:, j, :256], bank(bi, 256), [bk(bi)], [("fost", 0)])
                S.dma("sp", lambda fo=fo: nc.sync.dma_start(out=fos[4].rearrange("p (j t) -> p j t", j=4)[:, :, :256], in_=fo[:, :, :256]), [("fost", 0)], [("fos", 4)])
            S.free("fT", "AB", "tab", "fost")

            KT = salloc("KT", [128, 8, TA], BF16, at=R1)
            QT = salloc("QT", [128, 8, TA], BF16, at=R2)
            Vt = salloc("Vt", [128, 18, D], BF16, at=R3)
            TB = R4 + 16384
            NSET = 4
            qraw_b = [salloc("qraw%d" % i, [128, 512], F32, at=TB + i * 6144) for i in range(NSET)]
            qsq_b = [salloc("qsq%d" % i, [128, 512], BF16, at=TB + i * 6144 + 2048) for i in range(NSET)]
            qn_b = [salloc("qn%d" % i, [128, 512], BF16, at=TB + i * 6144 + 3072) for i in range(NSET)]
            qln_b = [salloc("qln%d" % i, [128, 512], F32, at=TB + i * 6144 + 4096) for i in range(NSET)]
            gst_b = qsq_b
            qst = {"n": 0}
            pending = []

            def tick(newgen=None):
                olds = list(pending)
                if newgen is not None:
                    try:
                        next(newgen)
                        pending.append(newgen)
                    except StopIteration:
                        pass
                for g_ in olds:
                    try:
                        next(g_)
                    except StopIteration:
                        pending.remove(g_)

            def qk_evac(bi, dst, dkey, gcol, gkey, h, c0, W, ti, isc):
                n_ = qst["n"]
                i = n_ % NSET
                qst["n"] += 1
                qraw, qsq, qln, qn = qraw_b[i], qsq_b[i], qln_b[i], qn_b[i]
                t1, t2 = qln, qraw
                S.op("act", lambda: A.activation(out=qraw[:, :W], in_=bank(bi, W), func=AF.Identity), [bk(bi)], [("qraw", i)])
                S.op("dve", lambda: V.tensor_tensor(out=qsq[:, :W], in0=qraw[:, :W], in1=qraw[:, :W], op=ALU.mult), [("qraw", i)], [("qsq", i)])
                yield
                b2 = nextbank(4, 8)
                S.op("pe", lambda: PE.matmul(bank(b2, W), lhsT=bdB, rhs=qsq[:, :W], start=True, stop=True), [("qsq", i), ("cb",)], [bk(b2)])
                S.op("act", lambda: A.activation(out=qln[:, :W], in_=bank(b2, W), func=AF.Ln, scale=1.0 / 64, bias=epsc[:]), [bk(b2), ("epsc",)], [("qln", i)])
                S.op("act", lambda: A.activation(out=qln[:, :W], in_=qln[:, :W], func=AF.Exp, scale=-0.5), [("qln", i)], [("qln", i)])
                if isc:
                    S.op("dve", lambda: V.scalar_tensor_tensor(out=dst[:, h, c0:c0 + W], in0=qraw[:, :W], scalar=gcol, in1=qln[:, :W], op0=ALU.mult, op1=ALU.mult),
                         [("qraw", i), ("qln", i), gkey], [(dkey, h, ti)])
                    return
                S.op("dve", lambda: V.scalar_tensor_tensor(out=qn[:, :W], in0=qraw[:, :W], scalar=gcol, in1=qln[:, :W], op0=ALU.mult, op1=ALU.mult),
                     [("qraw", i), ("qln", i), gkey], [("qn", i)])
                yield
                b3 = nextbank(4, 8)
                S.op("pe", lambda: PE.matmul(bank(b3, W), lhsT=rotB, rhs=qn[:, :W], start=True, stop=True), [("qn", i), ("cb",)], [bk(b3)])
                S.op("pool", lambda: G.tensor_tensor(out=t1[:, :W], in0=qn[:, :W], in1=cosT[:, c0:c0 + W], op=ALU.mult), [("qn", i), ("rope",)], [("qln", i)])
                S.op("dve", lambda: V.tensor_tensor(out=t2[:, :W], in0=bank(b3, W), in1=sinT[:, c0:c0 + W], op=ALU.mult), [bk(b3), ("rope",)], [("qraw", i)])
                yield
                eng, Eh = ("pool", G) if n_ % 2 == 0 else ("dve", V)
                S.op(eng, lambda: Eh.tensor_tensor(out=dst[:, h, c0:c0 + W], in0=t1[:, :W], in1=t2[:, :W], op=ALU.add), [("qln", i), ("qraw", i)], [(dkey, h, ti)])

            for g in range(2):
                wb, wk = load_wgroup(g * 512)
                for cbk in range(4):
                    h = g * 4 + cbk
                    for ti, (c0, W, isc) in enumerate(all_tiles):
                        bi = proj_block(wb, wk, cbk, c0, W, ti)
                        tick(qk_evac(bi, KT, "KT", kg[:, l:l + 1], ("kg",), h, c0, W, ti, isc))
            for g in range(2):
                wb, wk = load_wgroup(1024 + g * 512)
                for t in range(18):
                    ti = t // 4 if t < 16 else 4
                    bi = nextbank(0, 4)
                    for kc in range(8):
                        S.op("pe", lambda kc=kc, t=t, bi=bi, wb=wb: PE.matmul(bank(bi), lhsT=hT[:, kc, t * 128:(t + 1) * 128], rhs=wb[:, kc, :], start=(kc == 0), stop=(kc == 7)),
                             [wk, ("hT", kc, ti)], [bk(bi)])
                    tick()
                    copy_any(Vt[:, t, g * 512:(g + 1) * 512], bank(bi), [bk(bi)], [("Vt", t, g)])
            for g in range(2):
                wb, wk = load_wgroup(2048 + g * 512)
                for cbk in range(4):
                    h = g * 4 + cbk
                    for ti, (c0, W, isc) in enumerate(out_tiles):
                        bi = proj_block(wb, wk, cbk, c0, W, ti)
                        tick(qk_evac(bi, QT, "QT", qg[:, l:l + 1], ("qg",), h, c0, W, ti, isc))
            for gg in range(4):
                wb, wk = load_wgroup(3584 + gg * 512)
                for cbk in range(4):
                    ch = gg * 4 + cbk
                    for ti, (c0, W, isc) in enumerate(out_tiles):
                        bi = proj_block(wb, wk, cbk, c0, W, ti)
                        tick()
                        i = qst["n"] % NSET
                        qst["n"] += 1
                        gs = gst_b[i]
                        S.op("act", lambda gs=gs, bi=bi, W=W: A.activation(out=gs[:, :W], in_=bank(bi, W), func=AF.Sigmoid), [bk(bi)], [("qsq", i)])
                        S.dma("sp", lambda gs=gs, ch=ch, c0=c0, W=W: nc.sync.dma_start(out=gates[ti][:, ch * 512:ch * 512 + W], in_=gs[:, :W]), [("qsq", i)], [("gates", ch, ti)])
            while pending:
                tick()
            S.free("hT", "wb", "qraw", "qsq", "qln", "qn")

            oT = salloc("oT", [128, 8, TA], BF16, at=R0)
            PT_b = [salloc("PT%d" % i, [128, 1024], BF16, at=R4 + i * 2048) for i in range(3)]
            AT = R4 + 6144
            at_b = [[salloc("at%d_%d" % (i, k), [128, 512], BF16 if k == 5 else F32, at=AT + (i * 8 + k) * 2048) for k in range(8)] for i in range(2)]
            q_tiles = [(ti, c0, W, isc) for ti, (c0, W, isc) in enumerate(out_tiles)]
            items = []
            for h in range(NH):
                for (ti, c0, W, isc) in q_tiles:
                    kts = [16, 17] if isc else list(range(18))
                    for ki, kt in enumerate(kts):
                        items.append((h, ti, c0, W, isc, kt, ki == 0, ki == len(kts) - 1))

            def s_stage(n):
                h, ti, c0, W, isc, kt, first, last = items[n]
                kti = kt // 4 if kt < 16 else 4
                sa = (n % 2) * 2
                pi = n % 3
                PT = PT_b[pi]
                S.op("pe", lambda: PE.matmul(bank(sa, W), lhsT=KT[0:64, h, kt * 128:(kt + 1) * 128], rhs=QT[0:64, h, c0:c0 + W], start=True, stop=True),
                     [("KT", h, kti), ("QT", h, ti)], [bk(sa)])
                S.op("pe", lambda: PE.matmul(bank(sa + 1, W), lhsT=KT[64:128, h, kt * 128:(kt + 1) * 128], rhs=QT[64:128, h, c0:c0 + W], start=True, stop=True),
                     [("KT", h, kti), ("QT", h, ti)], [bk(sa + 1)])
                S.op("act", lambda: A.activation(out=PT[:].rearrange("p (i w) -> p i w", i=2)[:, :, :W],
                                                 in_=ps[:, sa * 512:(sa + 2) * 512].rearrange("p (i w) -> p i w", i=2)[:, :, :W], func=AF.Exp, scale=0.125),
                     [bk(sa), bk(sa + 1)], [("PT", pi)])

            grp = {"g": -1}

            def av_stage(n):
                h, ti, c0, W, isc, kt, first, last = items[n]
                pi = n % 3
                PT = PT_b[pi]
                if first:
                    grp["g"] += 1
                qi = grp["g"] % 2
                acc1 = at_b[qi][4]
                S.op("pe", lambda: PE.matmul(bank(4, W), lhsT=Vt[:, kt, h * 128:(h + 1) * 128], rhs=PT[:, 0:W], start=first, stop=last),
                     [("Vt", kt, h // 4), ("PT", pi)], [bk(4)])
                if first:
                    S.op("dve", lambda: V.tensor_copy(out=acc1[:, :W], in_=PT[:, 0:W]), [("PT", pi)], [("at", qi, 4)])
                else:
                    S.op("dve", lambda: V.tensor_tensor(out=acc1[:, :W], in0=acc1[:, :W], in1=PT[:, 0:W], op=ALU.add), [("PT", pi), ("at", qi, 4)], [("at", qi, 4)])
                S.op("pe", lambda: PE.matmul(bank(6, W), lhsT=Vt[:, kt, h * 128:(h + 1) * 128], rhs=PT[:, 512:512 + W], start=first, stop=last),
                     [("Vt", kt, h // 4), ("PT", pi)], [bk(6)])
                S.op("pe", lambda: PE.matmul(bank(7, W), lhsT=onesB, rhs=PT[:, 512:512 + W], start=first, stop=last),
                     [("cb",), ("PT", pi)], [bk(7)])

            def post1(n):
                h, ti, c0, W, isc, kt, first, last = items[n]
                qi = grp["g"] % 2
                l1, l2, o1, o2, acc1, acc1b = at_b[qi][0:6]
                kk = lambda k: ("at", qi, k)
                S.op("dve", lambda: V.tensor_copy(out=o1[:, :W], in_=bank(4, W)), [bk(4)], [kk(2)])
                S.op("act", lambda: A.activation(out=l2[:, :W], in_=bank(7, W), func=AF.Ln), [bk(7)], [kk(1)])
                S.op("dve", lambda: V.tensor_copy(out=o2[:, :W], in_=bank(6, W)), [bk(6)], [kk(3)])
                S.op("dve", lambda: V.tensor_copy(out=acc1b[:, :W], in_=acc1[:, :W]), [kk(4)], [kk(5)])
                S.op("act", lambda: A.activation(out=l2[:, :W], in_=l2[:, :W], func=AF.Exp, scale=-1.0), [kk(1)], [kk(1)])
                return (h, ti, c0, W, qi)

            def post2(info):
                h, ti, c0, W, qi = info
                l1, l2, o1, o2, acc1, acc1b = at_b[qi][0:6]
                kk = lambda k: ("at", qi, k)
                S.op("pe", lambda: PE.matmul(bank(5, W), lhsT=onesB, rhs=acc1b[:, :W], start=True, stop=True), [kk(5), ("cb",)], [bk(5)])
                S.op("act", lambda: A.activation(out=l1[:, :W], in_=bank(5, W), func=AF.Ln), [bk(5)], [kk(0)])
                S.op("act", lambda: A.activation(out=l1[:, :W], in_=l1[:, :W], func=AF.Exp, scale=-1.0), [kk(0)], [kk(0)])
                S.op("dve", lambda: V.tensor_tensor(out=o1[:, :W], in0=o1[:, :W], in1=l1[:, :W], op=ALU.mult), [kk(2), kk(0)], [kk(2)])
                S.op("dve", lambda: V.scalar_tensor_tensor(out=o2[:, :W], in0=o2[:, :W], scalar=neglam[:, l:l + 1], in1=l2[:, :W], op0=ALU.mult, op1=ALU.mult),
                     [kk(3), kk(1), ("neglam",)], [kk(3)])
                S.op("pool", lambda: G.tensor_tensor(out=oT[:, h, c0:c0 + W], in0=o2[:, :W], in1=o1[:, :W], op=ALU.add), [kk(2), kk(3)], [("oT", h, ti)])

            wpa = salloc("wpa", [128, 8, D], BF16, at=R1)
            wpf = salloc("wpf", [128, 4, D], BF16, at=R1 + 18432)
            wo = salloc("wo", [128, 8, D], BF16, at=R2)
            wpa_v = wpa_d[l].rearrange("(kc p) c -> p kc c", p=128)
            wpf_v = wpf_d[l].rearrange("(kc p) c -> p kc c", p=128)
            wo_v = wo_d[l].rearrange("(kc p) c -> p kc c", p=128)
            last_ti = q_tiles[-1][0]

            def dead(nm, heads):
                return [(nm, h_, t_) for h_ in heads for t_ in range(5)]

            s_stage(0)
            pend2 = []
            for n in range(len(items)):
                if n + 1 < len(items):
                    s_stage(n + 1)
                av_stage(n)
                while pend2:
                    post2(pend2.pop(0))
                if items[n][7]:
                    pend2.append(post1(n))
                    if items[n][1] == last_ti and items[n][0] == 3:
                        for hf in range(2):
                            S.dma("pool", lambda hf=hf: nc.gpsimd.dma_start(out=wpa[:, :, hf * 512:(hf + 1) * 512], in_=wpa_v[:, :, hf * 512:(hf + 1) * 512]), [], [("wpa", hf)] + dead("KT", range(4)))
                        for hf in range(2):
                            S.dma("pool", lambda hf=hf: nc.gpsimd.dma_start(out=wo[:, :, hf * 512:(hf + 1) * 512], in_=wo_v[:, :, hf * 512:(hf + 1) * 512]), [], [("wo", hf)] + dead("QT", range(4)))
                    if items[n][1] == last_ti and items[n][0] == 5:
                        for hf in range(2):
                            S.dma("pool", lambda hf=hf: nc.gpsimd.dma_start(out=wpf[:, :, hf * 512:(hf + 1) * 512], in_=wpf_v[:, :, hf * 512:(hf + 1) * 512]), [], [("wpf", hf)] + dead("KT", [4, 5]))
            while pend2:
                post2(pend2.pop(0))
            S.free("at", "PT", "KT", "QT", "Vt")
            M0 = R1 + 53248
            gat_b = [salloc("gat%d" % i, [128, 16, 512], BF16, at=M0 + i * 16384) for i in range(2)]
            mx_b = [salloc("mx%d" % i, [128, 8, 512], F32, at=M0 + 32768 + i * 16384) for i in range(2)]
            fot_b = [salloc("fot%d" % i, [128, 4, 512], BF16, at=M0 + 65536 + i * 4096) for i in range(2)]
            uT_b = [salloc("uT%d" % i, [128, 8, 512], BF16, at=M0 + 73728 + i * 8192) for i in range(2)]
            u_b = [salloc("u%d" % i, [128, 512], F32, at=M0 + 90112 + i * 2048) for i in range(4)]

            def p5_load(tix):
                c0, W, isc = out_tiles[tix]
                i = tix % 2
                gat, mx, fot = gat_b[i], mx_b[i], fot_b[i]
                S.dma("sp", lambda: nc.sync.dma_start(out=gat[:, :, :W], in_=gates[tix].rearrange("p (c t) -> p c t", c=16)[:, :, :W]),
                      [("gates", ch, tix) for ch in range(16)], [("gat", i)])
                S.dma("sp", lambda: nc.sync.dma_start(out=fot[:, :, :W], in_=fos[tix].rearrange("p (j t) -> p j t", j=4)[:, :, :W]), [("fos", tix)], [("fot", i)])
                S.dma("sp", lambda: nc.sync.dma_start(out=mx[:, :, :W], in_=xs_cols(c0, W)), [("xs", tix)], [("mx", i)])

            p5_load(0)
            SL0 = R1 + 124928
            osq_b = [salloc("osq%d" % i, [128, 4, 512], BF16, at=SL0 + i * 13312) for i in range(2)]
            oln_b = [salloc("oln%d" % i, [128, 4, 512], F32, at=SL0 + i * 13312 + 4096) for i in range(2)]
            units = [[(h, ti, c0, W) for (ti, c0, W, isc) in q_tiles if not isc] for h in range(NH)]
            if ctx_out:
                units += [[(h, 4, T, TC) for h in range(4)], [(h, 4, T, TC) for h in range(4, 8)]]
            for ui, unit in enumerate(units):
                si = ui % 2
                osq, oln = osq_b[si], oln_b[si]
                ne = len(unit)
                Wm = unit[0][3]
                for e, (h, ti, c0, W) in enumerate(unit):
                    eng, Eh = ("pool", G) if e % 2 == 0 else ("dve", V)
                    S.op(eng, lambda e=e, h=h, c0=c0, W=W, Eh=Eh: Eh.tensor_tensor(out=osq[:, e, :W], in0=oT[:, h, c0:c0 + W], in1=oT[:, h, c0:c0 + W], op=ALU.mult), [("oT", h, ti)], [("osq", si, e)])
                    S.op("pe", lambda e=e, W=W: PE.matmul(bank(si * 4 + e, W), lhsT=onesB, rhs=osq[:, e, :W], start=True, stop=True), [("osq", si, e), ("cb",)], [bk(si * 4 + e)])
                S.op("act", lambda: A.activation(out=oln[:, :ne, :Wm], in_=ps[:, si * 2048:(si + 1) * 2048].rearrange("p (e w) -> p e w", e=4)[:, :ne, :Wm], func=AF.Ln, scale=1.0 / 128, bias=epsc[:]),
                     [bk(si * 4 + e) for e in range(ne)] + [("epsc",)], [("oln", si)])
                S.op("act", lambda: A.activation(out=oln[:, :ne, :Wm], in_=oln[:, :ne, :Wm], func=AF.Exp, scale=-0.5), [("oln", si)], [("oln", si)])
                for e, (h, ti, c0, W) in enumerate(unit):
                    S.op("dve", lambda e=e, h=h, c0=c0, W=W: V.scalar_tensor_tensor(out=oT[:, h, c0:c0 + W], in0=oT[:, h, c0:c0 + W], scalar=gsub[:, l:l + 1], in1=oln[:, e, :W], op0=ALU.mult, op1=ALU.mult),
                         [("oT", h, ti), ("oln", si), ("gsub",)], [("oT", h, ti)])
            S.free("osq", "oln")

            ust = {"n": 0}
            for tix, (c0, W, isc) in enumerate(out_tiles):
                who = 2 if isc else b
                i = tix % 2
                gat, mx, uT, fot = gat_b[i], mx_b[i], uT_b[i], fot_b[i]
                if tix + 1 < len(out_tiles):
                    p5_load(tix + 1)
                for mc in range(8):
                    ba = nextbank(0, 8)
                    for kc in range(8):
                        S.op("pe", lambda kc=kc, mc=mc, ba=ba: PE.matmul(bank(ba, W), lhsT=wpa[:, kc, mc * 128:(mc + 1) * 128], rhs=oT[:, kc, c0:c0 + W], start=(kc == 0), stop=(kc == 7)),
                             [("wpa", mc // 4), ("oT", kc, tix)], [bk(ba)])
                    bf = nextbank(0, 8)
                    for kc in range(4):
                        S.op("pe", lambda kc=kc, mc=mc, bf=bf: PE.matmul(bank(bf, W), lhsT=wpf[:, kc, mc * 128:(mc + 1) * 128], rhs=fot[:, kc, :W], start=(kc == 0), stop=(kc == 3)),
                             [("wpf", mc // 4), ("fot", i)], [bk(bf)])
                    ui = ust["n"] % 2
                    ust["n"] += 1
                    u1, u2 = u_b[2 * ui], u_b[2 * ui + 1]
                    S.op("dve", lambda u1=u1, ba=ba, mc=mc: V.tensor_tensor(out=u1[:, :W], in0=bank(ba, W), in1=gat[:, mc, :W], op=ALU.mult), [bk(ba), ("gat", i)], [("u", 2 * ui)])
                    S.op("dve", lambda u2=u2, bf=bf, mc=mc: V.tensor_tensor(out=u2[:, :W], in0=bank(bf, W), in1=gat[:, 8 + mc, :W], op=ALU.mult), [bk(bf), ("gat", i)], [("u", 2 * ui + 1)])
                    S.op("pool", lambda u1=u1, u2=u2, mc=mc: G.tensor_tensor(out=uT[:, mc, :W], in0=u1[:, :W], in1=u2[:, :W], op=ALU.add), [("u", 2 * ui), ("u", 2 * ui + 1)], [("uT", i, mc)])
                for mc in range(8):
                    bz = nextbank(0, 8)
                    for kc in range(8):
                        S.op("pe", lambda kc=kc, mc=mc, bz=bz: PE.matmul(bank(bz, W), lhsT=wo[:, kc, mc * 128:(mc + 1) * 128], rhs=uT[:, kc, :W], start=(kc == 0), stop=(kc == 7)),
                             [("wo", mc // 4), ("uT", i, kc)], [bk(bz)])
                    S.op("dve", lambda mc=mc, bz=bz: V.scalar_tensor_tensor(out=mx[:, mc, :W], in0=bank(bz, W), scalar=mcol(l, 2, mc, who), in1=mx[:, mc, :W], op0=ALU.mult, op1=ALU.add),
                         [bk(bz), ("mx", i), ("modT",)], [("mx", i)])
                S.dma("sp", lambda mx=mx, c0=c0, W=W: nc.sync.dma_start(out=xs_cols(c0, W), in_=mx[:, :, :W]), [("mx", i)], [("xs", tix)])
            S.free("oT", "wpa", "wpf", "wo", "gat", "mx", "uT", "fot", "u")

            F0 = R1 + NJ * TA * 2
            wgu_b = [salloc("wgu%d" % i, [128, 8, 512], BF16, at=F0 + i * 8192) for i in range(2)]
            wd_b = [salloc("wdn%d" % i, [128, NJ, 256], BF16, at=F0 + 16384 + i * 11264) for i in range(2)]
            sg_b = [salloc("sgl%d" % i, [128, 512], F32, at=F0 + 38912 + i * 2048) for i in range(2)]
            fx_b = [salloc("fx%d" % i, [128, 512], F32, at=F0 + 43008 + i * 2048) for i in range(3)]
            assert F0 + 43008 + 3 * 2048 <= SB_END
            wgu_v = wgu_d[l].rearrange("(kc p) c -> p kc c", p=128)
            wd_v = wd_d[l].rearrange("(j p) c -> p j c", p=128)

            def load_wgu(jj):
                wi = jj % 2
                wg = wgu_b[wi]
                S.dma("pool", lambda: nc.gpsimd.dma_start(out=wg[:, :, 0:256], in_=wgu_v[:, :, jj * 256:(jj + 1) * 256]), [], [("wgu", wi, 0)])
                S.dma("pool", lambda: nc.gpsimd.dma_start(out=wg[:, :, 256:512], in_=wgu_v[:, :, DFF + jj * 256:DFF + (jj + 1) * 256]), [], [("wgu", wi, 1)])

            def load_wd(mp):
                wi = mp % 2
                wdn = wd_b[wi]
                S.dma("pool", lambda: nc.gpsimd.dma_start(out=wdn[:, 0:11, :], in_=wd_v[:, 0:11, mp * 256:(mp + 1) * 256]), [], [("wdn", wi, 0)])
                S.dma("pool", lambda: nc.gpsimd.dma_start(out=wdn[:, 11:22, :], in_=wd_v[:, 11:22, mp * 256:(mp + 1) * 256]), [], [("wdn", wi, 1)])

            load_wgu(0)
            hT = salloc("h2T", [128, 8, TA], BF16, at=R0)
            norm_phase(l, b, 1, out_tiles, hT)

            actT = salloc("actT", [128, NJ, TA], BF16, at=R1)
            fst = {"s": 0, "x": 0}
            for jj in range(NJ // 2):
                wi = jj % 2
                wg = wgu_b[wi]
                if jj + 1 < NJ // 2:
                    load_wgu(jj + 1)
                if jj == 6:
                    load_wd(0)
                if jj == 9:
                    load_wd(1)
                for jl in range(2):
                    j = jj * 2 + jl
                    for ti, (c0, W, isc) in enumerate(out_tiles):
                        bg = nextbank(0, 8)
                        for kc in range(8):
                            S.op("pe", lambda kc=kc, jl=jl, bg=bg, wg=wg, c0=c0, W=W: PE.matmul(bank(bg, W), lhsT=wg[:, kc, jl * 128:(jl + 1) * 128], rhs=hT[:, kc, c0:c0 + W], start=(kc == 0), stop=(kc == 7)),
                                 [("wgu", wi, 0), ("hT", kc, ti)], [bk(bg)])
                        bu = nextbank(0, 8)
                        for kc in range(8):
                            S.op("pe", lambda kc=kc, jl=jl, bu=bu, wg=wg, c0=c0, W=W: PE.matmul(bank(bu, W), lhsT=wg[:, kc, 256 + jl * 128:256 + (jl + 1) * 128], rhs=hT[:, kc, c0:c0 + W], start=(kc == 0), stop=(kc == 7)),
                                 [("wgu", wi, 1), ("hT", kc, ti)], [bk(bu)])
                        si = fst["s"] % 2
                        fst["s"] += 1
                        sgl = sg_b[si]
                        S.op("act", lambda sgl=sgl, bg=bg, W=W: A.activation(out=sgl[:, :W], in_=bank(bg, W), func=AF.Silu), [bk(bg)], [("sgl", si)])
                        S.op("dve", lambda sgl=sgl, bu=bu, j=j, c0=c0, W=W: V.tensor_tensor(out=actT[:, j, c0:c0 + W], in0=bank(bu, W), in1=sgl[:, :W], op=ALU.mult), [bk(bu), ("sgl", si)], [("actT", j, ti)])
            for mp in range(4):
                wi = mp % 2
                wdn = wd_b[wi]
                if mp >= 1 and mp + 1 < 4:
                    load_wd(mp + 1)
                for ml in range(2):
                    mc = mp * 2 + ml
                    for ti, (c0, W, isc) in enumerate(out_tiles):
                        who = 2 if isc else b
                        xi = fst["x"] % 3
                        fst["x"] += 1
                        fx = fx_b[xi]
                        S.dma("sp", lambda fx=fx, mc=mc, c0=c0, W=W: nc.sync.dma_start(out=fx[:, :W], in_=xs_chunk(mc, ti, W)), [("xs", ti)], [("fx", xi)])
                        bz = nextbank(0, 8)
                        for j in range(NJ):
                            S.op("pe", lambda j=j, ml=ml, bz=bz, wdn=wdn, c0=c0, W=W: PE.matmul(bank(bz, W), lhsT=wdn[:, j, ml * 128:(ml + 1) * 128], rhs=actT[:, j, c0:c0 + W], start=(j == 0), stop=(j == NJ - 1)),
                                 [("wdn", wi, j // 11), ("actT", j, ti)], [bk(bz)])
                        S.op("dve", lambda fx=fx, bz=bz, mc=mc, W=W, who=who: V.scalar_tensor_tensor(out=fx[:, :W], in0=bank(bz, W), scalar=mcol(l, 5, mc, who), in1=fx[:, :W], op0=ALU.mult, op1=ALU.add),
                             [bk(bz), ("fx", xi), ("modT",)], [("fx", xi)])
                        S.dma("sp", lambda fx=fx, mc=mc, c0=c0, W=W: nc.sync.dma_start(out=xs_chunk(mc, ti, W), in_=fx[:, :W]), [("fx", xi)], [("xs", ti)])
            S.free("hT", "actT", "wgu", "wdn", "sgl", "fx")

        xtt_b = [salloc("oxt%d" % i, [128, 8, 512], F32, at=R0 + i * 16384) for i in range(2)]
        xo_b = [salloc("oxo%d" % i, [128, 4, D], F32, at=R0 + 32768 + i * 16384) for i in range(2)]

        def p8_load(g):
            xtt = xtt_b[g % 2]
            S.dma("sp", lambda: nc.sync.dma_start(out=xtt[:], in_=xs_cols(g * 512, 512)), [("xs", g)], [("oxt", g % 2)])

        p8_load(0)
        p8_load(1)
        for g in range(4):
            xtt = xtt_b[g % 2]
            xo = xo_b[g % 2]
            for t in range(4):
                pb = 2 * t
                for c in range(8):
                    S.op("pe", lambda t=t, c=c, pb=pb: PE.transpose(out=ps[:, pb * 512 + c * 128:pb * 512 + (c + 1) * 128], in_=xtt[:, c, t * 128:(t + 1) * 128], identity=identF[:]),
                         [("oxt", g % 2), ("identF",)], [bk(pb + c // 4)])
                copy_any(xo[:, t, :], ps[:, pb * 512:pb * 512 + 1024], [bk(pb), bk(pb + 1)], [("oxo", g % 2, t)])
            if g + 2 < 4:
                p8_load(g + 2)
            S.dma("sp", lambda: nc.sync.dma_start(out=out_d[b, g * 512:(g + 1) * 512, :].rearrange("(t p) d -> p t d", p=128), in_=xo[:]),
                  [("oxo", g % 2, t) for t in range(4)], [("out", b, g)])
        S.free("oxt", "oxo")

    S.finish()
    return nc, S


_CONST = {}


def _consts():
    if _CONST:
        return _CONST
    bf = ml_dtypes.bfloat16
    ones = np.ones((128, 128), np.float32)
    bd = np.zeros((128, 128), np.float32)
    bd[:64, :64] = 1.0
    bd[64:, 64:] = 1.0
    rot = np.zeros((128, 128), np.float32)
    for m in range(128):
        d = m % 64
        half = (d // 16) % 2
        if half == 0:
            rot[m + 16, m] = -1.0
        else:
            rot[m - 16, m] = 1.0
    cidx = np.arange(128)
    angc = 2.0 * np.pi * ((cidx[:, None] * cidx[None, :]) % 128) / 128.0
    Cc = np.cos(angc)
    Scn = -np.sin(angc)
    _CONST["cbf"] = np.ascontiguousarray(np.stack([ones, bd, rot, Cc, Scn], axis=1)).astype(bf)
    _CONST["ident"] = np.eye(128, dtype=np.float32)
    t = np.arange(T)
    r = (t // 64).astype(np.float32)
    col = (t % 64).astype(np.float32)
    inv_freq = (10000.0 ** (-np.arange(0, 32, 2, dtype=np.float32) / 32.0)).astype(np.float32)
    ang_r = r[:, None] * inv_freq[None, :]
    ang_c = col[:, None] * inv_freq[None, :]
    ang = np.concatenate([ang_r, ang_r, ang_c, ang_c], axis=-1)
    cosT = np.cos(ang).T
    sinT = np.sin(ang).T
    rope = np.stack([np.concatenate([cosT, cosT], 0), np.concatenate([sinT, sinT], 0)], axis=1)
    _CONST["rope"] = np.ascontiguousarray(rope).astype(bf)
    tt = np.arange(T, dtype=np.int64)
    angt = 2.0 * np.pi * ((tt[:, None] * tt[None, :]) % T).astype(np.float64) / T
    sc = 1.0 / math.sqrt(T * 128.0)
    def _lay(tab):
        a = tab.astype(np.float32).reshape(2, 8, 128, 4, 512).transpose(3, 0, 2, 1, 4)
        return np.ascontiguousarray(a).reshape(4, 2, 128, 8 * 512).astype(bf)
    _CONST["ctab"] = _lay(np.cos(angt) * sc)
    _CONST["stab"] = _lay(np.sin(angt) * sc)
    t2 = np.arange(TC, dtype=np.int64)
    ang2 = 2.0 * np.pi * ((t2[:, None] * t2[None, :]) % TC).astype(np.float64) / TC
    sc2 = 1.0 / math.sqrt(TC * 128.0)
    c2 = (np.cos(ang2) * sc2).reshape(2, 128, TC).transpose(1, 0, 2)
    s2 = (np.sin(ang2) * sc2).reshape(2, 128, TC).transpose(1, 0, 2)
    _CONST["c256"] = np.ascontiguousarray(np.stack([c2, s2], axis=1)).astype(np.float32).astype(bf)
    return _CONST


def _cols(v):
    v = np.asarray(v, np.float32)
    n = v.shape[-1] // 128
    return np.ascontiguousarray(np.moveaxis(v.reshape(v.shape[:-1] + (n, 128)), -1, 0))


_PROG = {}


def _program():
    if "nc" not in _PROG:
        _, S1 = build_program(None)
        rank = {e: {v: i + 1 for i, v in enumerate(sorted(S1.needed[e]))} for e in Sched.ENGS}
        nc, _ = build_program(rank)
        _PROG["nc"] = nc
    return _PROG["nc"]


def make_in_maps(inp, ncores=8):
    C = _consts()
    f = lambda a: np.ascontiguousarray(np.asarray(a, np.float32))
    shared = {
        "w_ada": f(inp["w_ada"]), "w_in": f(inp["w_in"]), "w_proj_attn": f(inp["w_proj_attn"]),
        "w_proj_fourier": f(inp["w_proj_fourier"]), "w_out": f(inp["w_out"]),
        "w_gate_up": f(inp["w_gate_up"]), "w_down": f(inp["w_down"]),
        "bada": _cols(inp["b_ada"]), "n1g": _cols(inp["norm1_g"]), "n2g": _cols(inp["norm2_g"]),
        "qg": np.ascontiguousarray(np.tile(f(inp["q_norm_g"]), (1, 2)).T),
        "kg": np.ascontiguousarray(np.tile(f(inp["k_norm_g"]), (1, 2)).T),
        "sublng": np.ascontiguousarray(f(inp["subln_g"]).T),
        "lamv": np.ascontiguousarray(np.broadcast_to(
            np.stack([f(inp["lambda_q1"]), f(inp["lambda_k1"]), f(inp["lambda_q2"]), f(inp["lambda_k2"])], axis=1)[None], (128, DEPTH, 4, 64))),
        "cbf": C["cbf"], "ident": C["ident"], "rope": C["rope"], "c256": C["c256"], "ctab": C["ctab"], "stab": C["stab"],
    }
    x = f(inp["x"])
    ctx = f(inp["ctx"])
    c = f(inp["c"])
    cc = f(inp["c_ctx"])
    maps = []
    for k in range(ncores):
        cv = np.stack([c[2 * k], c[2 * k + 1], cc, cc], axis=0)
        m = dict(shared)
        m["x2"] = x[2 * k:2 * k + 2]
        m["ctx2"] = ctx[2 * k:2 * k + 2]
        m["cT"] = np.ascontiguousarray(cv.reshape(4, 8, 128).transpose(2, 1, 0))
        maps.append(m)
    return maps


def kernel(**inputs):
    nc = _program()
    maps = make_in_maps(inputs, 8)
    res = run_bass_kernel_spmd(nc, maps, core_ids=list(range(8)))
    return np.concatenate([np.asarray(r["out"], np.float32) for r in res.results], axis=0)
```

```python
import math
import numpy as np
import ml_dtypes
import concourse.bass as bass
import concourse.mybir as mybir
from concourse.bass_utils import run_bass_kernel_spmd

F32 = mybir.dt.float32
BF16 = mybir.dt.bfloat16
AF = mybir.ActivationFunctionType
ALU = mybir.AluOpType
AX = mybir.AxisListType

D = 1024
T = 2048
TC = 256
TA = T + TC
NH = 8
DFF = 2816
NJ = DFF // 128
INW = 5632
EPS = 1e-6
DEPTH = 2
NB = 2
LAM_INIT = [0.8 - 0.6 * math.exp(-0.3 * l) for l in range(DEPTH)]
TILES = [(0, 512, 0), (512, 512, 0), (1024, 512, 0), (1536, 512, 0), (2048, 256, 1)]

SAME_SYNC = True
SB0 = 16512
SB_END = 229344


class Sched:
    EPOCH = 4000
    R = 8
    ENGS = ("pe", "act", "dve", "pool")

    def __init__(self, nc, rank=None):
        self.nc = nc
        self.rank = rank
        self.h = {"pe": nc.tensor, "act": nc.scalar, "dve": nc.vector, "pool": nc.gpsimd, "sp": nc.sync}
        self.idx = {e: 0 for e in self.ENGS}
        self.needed = {e: set() for e in self.ENGS}
        self.waited = {}
        self.state = {}
        self.released = {}
        self.dman = {"sp": 0, "pool": 0, "act": 0}
        self.dcount = {}
        self.esems = {e: [] for e in self.ENGS}
        self.dsems = {}
        self.same_sync = SAME_SYNC
        if rank is not None:
            for e in self.ENGS:
                n = len(rank[e])
                for k in range((n + self.EPOCH - 1) // self.EPOCH + 1):
                    self.esems[e].append(nc.alloc_semaphore("s_%s_%d" % (e, k)))
            for q in ("sp", "pool", "act"):
                for i in range(self.R):
                    self.dsems[(q, i)] = nc.alloc_semaphore("d_%s_%d" % (q, i))

    def _st(self, k):
        st = self.state.get(k)
        if st is None:
            st = [None, dict(self.released)]
            self.state[k] = st
        return st

    def free(self, *names):
        for k in list(self.state.keys()):
            if k[0] in names:
                st = self.state.pop(k)
                if st[0] is not None:
                    p, v = st[0]
                    if self.released.get(p, 0) < v:
                        self.released[p] = v
                for p, v in st[1].items():
                    if self.released.get(p, 0) < v:
                        self.released[p] = v

    def _deps(self, reads, writes):
        deps = {}
        for k in reads:
            st = self._st(k)
            if st[0] is not None:
                p, v = st[0]
                if deps.get(p, 0) < v:
                    deps[p] = v
        for k in writes:
            st = self._st(k)
            if st[0] is not None:
                p, v = st[0]
                if deps.get(p, 0) < v:
                    deps[p] = v
            for p, v in st[1].items():
                if deps.get(p, 0) < v:
                    deps[p] = v
        return deps

    def _wait(self, eng, deps):
        for p, v in deps.items():
            if p == eng and (eng == "pe" or not self.same_sync):
                continue
            key = (eng, p)
            if self.waited.get(key, 0) >= v:
                continue
            self.waited[key] = v
            if isinstance(p, str):
                self.needed[p].add(v)
                if self.rank is not None:
                    r = self.rank[p][v]
                    self.h[eng].wait_ge(self.esems[p][(r - 1) // self.EPOCH], (r - 1) % self.EPOCH + 1)
            else:
                if self.rank is not None:
                    self.h[eng].wait_ge(self.dsems[p], v)

    def _record(self, ev, reads, writes):
        for k in writes:
            st = self._st(k)
            st[0] = ev
            st[1] = {}
        p, v = ev
        for k in reads:
            st = self._st(k)
            if st[1].get(p, 0) < v:
                st[1][p] = v

    def op(self, eng, fn, reads=(), writes=()):
        self._wait(eng, self._deps(reads, writes))
        i = self.idx[eng] + 1
        self.idx[eng] = i
        inst = fn()
        if self.rank is not None:
            r = self.rank[eng].get(i)
            if r is not None:
                inst.then_inc(self.esems[eng][(r - 1) // self.EPOCH], 1)
        self._record((eng, i), reads, writes)

    def dma(self, q, fn, reads=(), writes=()):
        deps = self._deps(reads, writes)
        n = self.dman[q]
        self.dman[q] = n + 1
        prod = (q, n % self.R)
        prev = self.dcount.get(prod, 0)
        if prev > 0 and deps.get(prod, 0) < prev:
            deps[prod] = prev
        self._wait(q, deps)
        cnt = prev + 16
        self.dcount[prod] = cnt
        inst = fn()
        if self.rank is not None:
            inst.then_inc(self.dsems[prod], 16)
        self._record((prod, cnt), reads, writes)

    def finish(self):
        if self.rank is None:
            for e in self.ENGS:
                if self.idx[e] > 0:
                    self.needed[e].add(self.idx[e])
            return
        for prod, cnt in self.dcount.items():
            self.nc.sync.wait_ge(self.dsems[prod], cnt)
        for e in self.ENGS:
            if self.idx[e] > 0:
                r = self.rank[e][self.idx[e]]
                self.nc.sync.wait_ge(self.esems[e][(r - 1) // self.EPOCH], (r - 1) % self.EPOCH + 1)


def build_program(rank=None, nb=NB, depth=DEPTH, dbg=None):
    nc = bass.Bass("TRN2", target_bir_lowering=False)
    S = Sched(nc, rank)

    def din(name, shape, dt=F32):
        return nc.dram_tensor(name, list(shape), dt, kind="ExternalInput").ap()

    x_d = din("x2", [NB, T, D])
    ctx_d = din("ctx2", [NB, TC, D])
    cT_d = din("cT", [128, 8, 4])
    wada_d = din("w_ada", [DEPTH, D, 6 * D])
    bada_d = din("bada", [128, DEPTH, 48])
    n1g_d = din("n1g", [128, DEPTH, 8])
    n2g_d = din("n2g", [128, DEPTH, 8])
    win_d = din("w_in", [DEPTH, D, INW])
    qg_d = din("qg", [128, DEPTH])
    kg_d = din("kg", [128, DEPTH])
    sg_d = din("sublng", [128, DEPTH])
    lamv_d = din("lamv", [128, DEPTH, 4, 64])
    wpa_d = din("w_proj_attn", [DEPTH, D, D])
    wpf_d = din("w_proj_fourier", [DEPTH, 512, D])
    wo_d = din("w_out", [DEPTH, D, D])
    wgu_d = din("w_gate_up", [DEPTH, D, 2 * DFF])
    wd_d = din("w_down", [DEPTH, DFF, D])
    cb_d = din("cbf", [128, 5, 128], BF16)
    ident_d = din("ident", [128, 128])
    rope_d = din("rope", [128, 2, T], BF16)
    c256_d = din("c256", [128, 2, 2, 256], BF16)
    ctab_d = din("ctab", [4, 2, 128, 8 * 512], BF16)
    stab_d = din("stab", [4, 2, 128, 8 * 512], BF16)
    out_d = nc.dram_tensor("out", [NB, T, D], F32, kind="ExternalOutput").ap()
    xs = nc.dram_tensor("xs", [5, 128, 8 * 512], F32, kind="Internal").ap()
    gates = nc.dram_tensor("gates", [5, 128, 16 * 512], BF16, kind="Internal").ap()
    fos = nc.dram_tensor("fos", [5, 128, 4 * 512], BF16, kind="Internal").ap()
    dbg_d = {}
    if dbg:
        for name, shape, dt in dbg:
            dbg_d[name] = nc.dram_tensor("dbg_" + name, list(shape), dt, kind="ExternalOutput").ap()

    cur = [SB0]

    def salloc(name, shape, dt, at=None):
        nbytes = int(np.prod(shape[1:])) * (4 if dt == F32 else 2)
        if at is None:
            off = cur[0]
            cur[0] = (off + nbytes + 31) // 32 * 32
        else:
            off = at
        assert off % 32 == 0 and off + nbytes <= SB_END, (name, off, nbytes)
        return nc.alloc_sbuf_tensor_at(name, list(shape), dt, offset=off)

    identF = salloc("identF", [128, 128], F32)
    cb = salloc("cb", [128, 5, 128], BF16)
    rope = salloc("rope", [128, 2, T], BF16)
    c256 = salloc("c256", [128, 2, 2, 256], BF16)
    cT = salloc("cT", [128, 8, 4], F32)
    scT = salloc("scT", [128, 8, 4], F32)
    bada = salloc("bada", [128, DEPTH, 48], F32)
    n1g = salloc("n1g", [128, DEPTH, 8], F32)
    n2g = salloc("n2g", [128, DEPTH, 8], F32)
    qg = salloc("qg", [128, DEPTH], F32)
    kg = salloc("kg", [128, DEPTH], F32)
    sg = salloc("sg", [128, DEPTH], F32)
    lamv = salloc("lamv", [128, DEPTH, 4, 64], F32)
    modT = salloc("modT", [128, DEPTH, 48, 4], F32)
    Gm = salloc("Gm", [128, DEPTH, 3, 2, 8], F32)
    neglam = salloc("neglam", [128, DEPTH], F32)
    gsub = salloc("gsub", [128, DEPTH], F32)
    epsc = salloc("epsc", [128, 1], F32)
    lamt = salloc("lamt", [128, 8], F32)
    lamp = salloc("lamp", [128, 64], F32)
    A0 = cur[0]
    ARENA = SB_END - A0
    assert ARENA >= 190000, ARENA
    R0 = A0
    R1 = A0 + 36864
    R2 = A0 + 73728
    R3 = A0 + 110592
    R4 = A0 + 147456

    onesB = cb[:, 0, :]
    bdB = cb[:, 1, :]
    rotB = cb[:, 2, :]
    CcB = cb[:, 3, :]
    ScnB = cb[:, 4, :]
    cosT = rope[:, 0, :]
    sinT = rope[:, 1, :]

    ps = nc.alloc_psum_tensor("ps", [128, 4096], F32)

    def bank(i, w=512):
        return ps[:, i * 512:i * 512 + w]

    def bk(i):
        return ("ps", i)

    V = nc.vector
    A = nc.scalar
    G = nc.gpsimd
    PE = nc.tensor

    cnt = {"b": 0, "cp": 0}

    def nextbank(lo, hi):
        n = hi - lo
        cnt["b"] += 1
        return lo + cnt["b"] % n

    def copy_any(out, in_, reads, writes):
        cnt["cp"] += 1
        if cnt["cp"] % 2:
            S.op("act", lambda: A.activation(out=out, in_=in_, func=AF.Identity), reads, writes)
        else:
            S.op("dve", lambda: V.tensor_copy(out=out, in_=in_), reads, writes)

    def ld(q, out, in_, wkey, rkeys=()):
        S.dma(q, lambda: (nc.sync if q == "sp" else nc.gpsimd).dma_start(out=out, in_=in_), rkeys, [wkey])

    ld("sp", identF[:], ident_d, ("identF",))
    ld("sp", cb[:], cb_d, ("cb",))
    ld("sp", rope[:], rope_d, ("rope",))
    ld("sp", c256[:], c256_d, ("c256",))
    ld("sp", cT[:], cT_d, ("cT",))
    ld("sp", bada[:], bada_d, ("bada",))
    ld("sp", n1g[:], n1g_d, ("n1g",))
    ld("sp", n2g[:], n2g_d, ("n2g",))
    ld("sp", qg[:], qg_d, ("qg",))
    ld("sp", kg[:], kg_d, ("kg",))
    ld("sp", sg[:], sg_d, ("sg",))
    ld("sp", lamv[:], lamv_d, ("lamv",))
    S.op("dve", lambda: V.memset(epsc[:], EPS), [], [("epsc",)])
    S.op("act", lambda: A.activation(out=scT[:], in_=cT[:], func=AF.Silu), [("cT",)], [("scT",)])

    wa_bufs = [salloc("wa%d" % i, [128, 8, 512], BF16, at=R0 + i * 8192) for i in range(3)]
    scTb = salloc("scTb", [128, 8, 4], BF16, at=R0 + 3 * 8192)
    S.op("dve", lambda: V.tensor_copy(out=scTb[:], in_=scT[:]), [("scT",)], [("scTb",)])
    wada_v = [wada_d[l].rearrange("(kc p) c -> p kc c", p=128) for l in range(DEPTH)]
    gi = 0
    for l in range(depth):
        for jg in range(12):
            wa = wa_bufs[gi % 3]
            wk = ("wa", gi % 3)
            gi += 1
            S.dma("pool", lambda wa=wa, l=l, jg=jg: nc.gpsimd.dma_start(out=wa[:], in_=wada_v[l][:, :, jg * 512:(jg + 1) * 512]), [], [wk])
            for jb in range(4):
                j = jg * 4 + jb
                bi = nextbank(0, 8)
                for kc in range(8):
                    S.op("pe", lambda wa=wa, jb=jb, kc=kc, bi=bi: PE.matmul(bank(bi, 4), lhsT=wa[:, kc, jb * 128:(jb + 1) * 128], rhs=scTb[:, kc, :], start=(kc == 0), stop=(kc == 7)),
                         [wk, ("scTb",)], [bk(bi)])
                S.op("dve", lambda l=l, j=j, bi=bi: V.tensor_scalar(out=modT[:, l, j, :], in0=bank(bi, 4), scalar1=bada[:, l, j:j + 1], scalar2=None, op0=ALU.add),
                     [bk(bi), ("bada",)], [("modT",)])
        for who in range(3):
            S.op("dve", lambda l=l, who=who: V.scalar_tensor_tensor(out=Gm[:, l, who, 0, :], in0=modT[:, l, 8:16, who], scalar=1.0, in1=n1g[:, l, :], op0=ALU.add, op1=ALU.mult),
                 [("modT",), ("n1g",)], [("Gm",)])
            S.op("dve", lambda l=l, who=who: V.scalar_tensor_tensor(out=Gm[:, l, who, 1, :], in0=modT[:, l, 32:40, who], scalar=1.0, in1=n2g[:, l, :], op0=ALU.add, op1=ALU.mult),
                 [("modT",), ("n2g",)], [("Gm",)])
        for i in range(2):
            S.op("dve", lambda l=l, i=i: V.tensor_tensor(out=lamp[:], in0=lamv[:, l, 2 * i, :], in1=lamv[:, l, 2 * i + 1, :], op=ALU.mult), [("lamv",)], [("lamp",)])
            S.op("dve", lambda i=i: V.reduce_sum(out=lamt[:, i:i + 1], in_=lamp[:], axis=AX.X), [("lamp",)], [("lamt",)])
        S.op("act", lambda: A.activation(out=lamt[:, 2:4], in_=lamt[:, 0:2], func=AF.Exp), [("lamt",)], [("lamt",)])
        S.op("dve", lambda l=l: V.tensor_tensor(out=lamt[:, 4:5], in0=lamt[:, 3:4], in1=lamt[:, 2:3], op=ALU.subtract), [("lamt",)], [("lamt",)])
        S.op("dve", lambda l=l: V.tensor_scalar(out=neglam[:, l:l + 1], in0=lamt[:, 4:5], scalar1=-LAM_INIT[l], scalar2=None, op0=ALU.add), [("lamt",)], [("neglam",)])
        S.op("dve", lambda l=l: V.tensor_scalar(out=gsub[:, l:l + 1], in0=sg[:, l:l + 1], scalar1=(1.0 - LAM_INIT[l]), scalar2=None, op0=ALU.mult), [("sg",)], [("gsub",)])
    S.free("wa", "scTb")

    def mcol(l, part, c, who):
        return modT[:, l, part * 8 + c, who:who + 1]

    def xs_cols(c0, w):
        return xs[c0 // 512].rearrange("p (c t) -> p c t", c=8)[:, :, :w]

    def xs_chunk(mc, ti, w):
        return xs[ti][:, mc * 512:mc * 512 + w]

    def norm_phase(l, b, which, tiles, hT):
        xt_b = [salloc("nx%d" % i, [128, 8, 512], F32, at=R1 + i * 16384) for i in range(3)]
        sq_b = [salloc("nsq%d" % i, [128, 8, 512], BF16, at=R1 + 49152 + i * 8192) for i in range(2)]
        ln_b = [salloc("nln%d" % i, [128, 512], F32, at=R1 + 65536 + i * 2048) for i in range(2)]
        rs_b = [salloc("nrs%d" % i, [128, 512], F32, at=R1 + 69632 + i * 2048) for i in range(2)]
        for ti, (c0, W, isc) in enumerate(tiles):
            who = 2 if isc else b
            xt = xt_b[ti % 3]
            sq = sq_b[ti % 2]
            ln = ln_b[ti % 2]
            rs = rs_b[ti % 2]
            kx, ksq, kln, krs = ("nx", ti % 3), ("nsq", ti % 2), ("nln", ti % 2), ("nrs", ti % 2)
            S.dma("sp", lambda xt=xt, c0=c0, W=W: nc.sync.dma_start(out=xt[:, :, :W], in_=xs_cols(c0, W)), [("xs", ti)], [kx])
            S.op("pool", lambda xt=xt, sq=sq, W=W: G.tensor_tensor(out=sq[:, 0:4, :W], in0=xt[:, 0:4, :W], in1=xt[:, 0:4, :W], op=ALU.mult), [kx], [(ksq[0], ksq[1], 0)])
            S.op("dve", lambda xt=xt, sq=sq, W=W: V.tensor_tensor(out=sq[:, 4:8, :W], in0=xt[:, 4:8, :W], in1=xt[:, 4:8, :W], op=ALU.mult), [kx], [(ksq[0], ksq[1], 1)])
            bi = nextbank(0, 8)
            for c in range(8):
                S.op("pe", lambda sq=sq, c=c, bi=bi, W=W: PE.matmul(bank(bi, W), lhsT=onesB, rhs=sq[:, c, :W], start=(c == 0), stop=(c == 7)), [(ksq[0], ksq[1], c // 4), ("cb",)], [bk(bi)])
            S.op("act", lambda ln=ln, bi=bi, W=W: A.activation(out=ln[:, :W], in_=bank(bi, W), func=AF.Ln, scale=1.0 / D, bias=epsc[:]), [bk(bi), ("epsc",)], [kln])
            S.op("act", lambda ln=ln, rs=rs, W=W: A.activation(out=rs[:, :W], in_=ln[:, :W], func=AF.Exp, scale=-0.5), [kln], [krs])
            S.op("dve", lambda xt=xt, rs=rs, W=W: V.tensor_tensor(out=xt[:, :, :W], in0=xt[:, :, :W], in1=rs[:, :W].unsqueeze(1).broadcast_to([128, 8, W]), op=ALU.mult), [kx, krs], [kx])
            for c in range(8):
                S.op("act", lambda xt=xt, c=c, c0=c0, W=W, who=who: A.activation(out=hT[:, c, c0:c0 + W], in_=xt[:, c, :W], func=AF.Identity,
                                                                                 scale=Gm[:, l, who, which, c:c + 1], bias=mcol(l, 3 * which, c, who)),
                     [kx, ("Gm",), ("modT",)], [("hT", c, ti)])
        S.free("nx", "nsq", "nln", "nrs")

    for b in range(nb):
        xin_b = [salloc("xin%d" % i, [128, 4, D], F32, at=R0 + i * 16384) for i in range(2)]
        xtt_b = [salloc("xtt%d" % i, [128, 8, 512], F32, at=R0 + 32768 + i * 16384) for i in range(2)]

        def p0_load(g):
            c0, W, isc = TILES[g]
            nt = W // 128
            xin = xin_b[g % 2]
            src = (ctx_d[b] if isc else x_d[b, c0:c0 + W, :]).rearrange("(t p) d -> p t d", p=128)
            S.dma("sp", lambda: nc.sync.dma_start(out=xin[:, :nt, :], in_=src), [], [("xin", g % 2)])

        p0_load(0)
        p0_load(1)
        for g, (c0, W, isc) in enumerate(TILES):
            nt = W // 128
            xin = xin_b[g % 2]
            xtt = xtt_b[g % 2]
            for t in range(nt):
                pb = 2 * t
                for j in range(8):
                    S.op("pe", lambda t=t, j=j, pb=pb: PE.transpose(out=ps[:, pb * 512 + j * 128:pb * 512 + (j + 1) * 128], in_=xin[:, t, j * 128:(j + 1) * 128], identity=identF[:]),
                         [("xin", g % 2), ("identF",)], [bk(pb + j // 4)])
                copy_any(xtt[:, :, t * 128:(t + 1) * 128], ps[:, pb * 512:pb * 512 + 1024].rearrange("p (c t) -> p c t", c=8), [bk(pb), bk(pb + 1)], [("xtt", g % 2, t)])
            if g + 2 < len(TILES):
                p0_load(g + 2)
            S.dma("sp", lambda: nc.sync.dma_start(out=xs_cols(c0, W), in_=xtt[:, :, :W]), [("xtt", g % 2, t) for t in range(nt)], [("xs", g)])
        S.free("xin", "xtt")

        for l in range(depth):
            ctx_out = l < DEPTH - 1
            all_tiles = TILES
            lat_tiles = TILES[:4]
            out_tiles = TILES if ctx_out else lat_tiles

            wbufs = [salloc("wb%d" % i, [128, 8, 512], BF16, at=R4 + i * 8192) for i in range(2)]
            win_v = win_d[l].rearrange("(kc p) c -> p kc c", p=128)
            wsched = [3072, 0, 512, 1024, 1536, 2048, 2560, 3584, 4096, 4608, 5120]
            wst = {"issued": 0, "used": 0}

            def issue_wload():
                g_ = wst["issued"]
                if g_ >= len(wsched):
                    return
                wst["issued"] += 1
                i = g_ % 2
                wb = wbufs[i]
                col0 = wsched[g_]
                S.dma("pool", lambda: nc.gpsimd.dma_start(out=wb[:], in_=win_v[:, :, col0:col0 + 512]), [], [("wb", i)])

            def load_wgroup(col0):
                g_ = wst["used"]
                assert wsched[g_] == col0
                wst["used"] += 1
                while wst["issued"] <= g_ + 1:
                    if wst["issued"] >= len(wsched):
                        break
                    issue_wload()
                i = g_ % 2
                return wbufs[i], ("wb", i)

            issue_wload()
            hT = salloc("hT", [128, 8, TA], BF16, at=R0)
            norm_phase(l, b, 0, all_tiles, hT)

            def proj_block(wb, wk, cbk, c0, W, ti, lo=0, hi=4):
                bi = nextbank(lo, hi)
                for kc in range(8):
                    S.op("pe", lambda kc=kc: PE.matmul(bank(bi, W), lhsT=wb[:, kc, cbk * 128:(cbk + 1) * 128], rhs=hT[:, kc, c0:c0 + W], start=(kc == 0), stop=(kc == 7)),
                         [wk, ("hT", kc, ti)], [bk(bi)])
                return bi

            fT = salloc("fT", [128, 4, TA], BF16, at=R3)
            wb, wk = load_wgroup(3072)
            for cbk in range(4):
                for ti, (c0, W, isc) in enumerate(out_tiles):
                    bi = proj_block(wb, wk, cbk, c0, W, ti)
                    copy_any(fT[:, cbk, c0:c0 + W], bank(bi, W), [bk(bi)], [("fT", cbk, ti)])

            AB = salloc("AB", [128, 18, 1024], BF16, at=R1)
            tabs = [[salloc("tab%d%d" % (s_, hf), [128, 8, 512], BF16, at=R2 + (s_ * 2 + hf) * 8192) for hf in range(2)] for s_ in range(2)]
            fost = [salloc("fost%d" % i, [128, 4, 512], BF16, at=R3 + 18432 + i * 4096) for i in range(2)]
            tab_v = [ctab_d, stab_d]

            def load_tabs(tq, hf):
                for s_ in range(2):
                    S.dma("sp", lambda s_=s_: nc.sync.dma_start(out=tabs[s_][hf][:].rearrange("p t q -> p (t q)"), in_=tab_v[s_][tq, hf]), [], [("tab", s_, hf)])

            load_tabs(0, 0)
            load_tabs(0, 1)
            ntt = 18 if ctx_out else 16
            for t in range(ntt):
                ti = t // 4 if t < 16 else 4
                ba = 2 * (t % 4)
                for g in range(4):
                    S.op("pe", lambda t=t, g=g, ba=ba: PE.matmul(ps[:, ba * 512 + g * 128:ba * 512 + (g + 1) * 128], lhsT=fT[:, g, t * 128:(t + 1) * 128], rhs=CcB, start=True, stop=True),
                         [("fT", g, ti), ("cb",)], [bk(ba)])
                for g in range(4):
                    S.op("pe", lambda t=t, g=g, ba=ba: PE.matmul(ps[:, (ba + 1) * 512 + g * 128:(ba + 1) * 512 + (g + 1) * 128], lhsT=fT[:, g, t * 128:(t + 1) * 128], rhs=ScnB, start=True, stop=True),
                         [("fT", g, ti), ("cb",)], [bk(ba + 1)])
                copy_any(AB[:, t, :], ps[:, ba * 512:ba * 512 + 1024], [bk(ba), bk(ba + 1)], [("AB", t)])
            for tq in range(4):
                c0 = tq * 512
                ab = (tq % 2) * 4
                for hf in range(2):
                    for j in range(4):
                        for tl in range(8):
                            t = hf * 8 + tl
                            for s_ in range(2):
                                S.op("pe", lambda j=j, t=t, tl=tl, s_=s_, hf=hf, ab=ab: PE.matmul(bank(ab + j), lhsT=AB[:, t, s_ * 512 + j * 128:s_ * 512 + (j + 1) * 128], rhs=tabs[s_][hf][:, tl, :],
                                                                                          start=(t == 0 and s_ == 0), stop=(t == 15 and s_ == 1)),
                                     [("AB", t), ("tab", s_, hf)], [bk(ab + j)])
                    if tq + 1 < 4:
                        load_tabs(tq + 1, hf)
                fo = fost[tq % 2]
                for j in range(4):
                    copy_any(fo[:, j, :], bank(ab + j), [bk(ab + j)], [("fost", tq % 2)])
                S.dma("sp", lambda fo=fo, c0=c0: nc.sync.dma_start(out=fos[tq].rearrange("p (j t) -> p j t", j=4), in_=fo[:]), [("fost", tq % 2)], [("fos", tq)])
            if ctx_out:
                fo = fost[0]
                for j in range(4):
                    bi = nextbank(0, 8)
                    for tl in range(2):
                        for s_ in range(2):
                            S.op("pe", lambda j=j, tl=tl, s_=s_, bi=bi: PE.matmul(bank(bi, 256), lhsT=AB[:, 16 + tl, s_ * 512 + j * 128:s_ * 512 + (j + 1) * 128], rhs=c256[:, s_, tl, :],
                                                                                  start=(tl == 0 and s_ == 0), stop=(tl == 1 and s_ == 1)),
                                 [("AB", 16 + tl), ("c256",)], [bk(bi)])
                    copy_any(fo[:, j, :256], bank(bi, 256), [bk(bi)], [("fost", 0)])
                S.dma("sp", lambda fo=fo: nc.sync.dma_start(out=fos[4].rearrange("p (j t) -> p j t", j=4)[:, :, :256], in_=fo[:, :, :256]), [("fost", 0)], [("fos", 4)])
            S.free("fT", "AB", "tab", "fost")

            KT = salloc("KT", [128, 8, TA], BF16, at=R1)
            QT = salloc("QT", [128, 8, TA], BF16, at=R2)
            Vt = salloc("Vt", [128, 18, D], BF16, at=R3)
            TB = R4 + 16384
            NSET = 4
            qraw_b = [salloc("qraw%d" % i, [128, 512], F32, at=TB + i * 6144) for i in range(NSET)]
            qsq_b = [salloc("qsq%d" % i, [128, 512], BF16, at=TB + i * 6144 + 2048) for i in range(NSET)]
            qn_b = [salloc("qn%d" % i, [128, 512], BF16, at=TB + i * 6144 + 3072) for i in range(NSET)]
            qln_b = [salloc("qln%d" % i, [128, 512], F32, at=TB + i * 6144 + 4096) for i in range(NSET)]
            gst_b = qsq_b
            qst = {"n": 0}
            pending = []

            def tick(newgen=None):
                olds = list(pending)
                if newgen is not None:
                    try:
                        next(newgen)
                        pending.append(newgen)
                    except StopIteration:
                        pass
                for g_ in olds:
                    try:
                        next(g_)
                    except StopIteration:
                        pending.remove(g_)

            def qk_evac(bi, dst, dkey, gcol, gkey, h, c0, W, ti, isc):
                n_ = qst["n"]
                i = n_ % NSET
                qst["n"] += 1
                qraw, qsq, qln, qn = qraw_b[i], qsq_b[i], qln_b[i], qn_b[i]
                t1, t2 = qln, qraw
                S.op("act", lambda: A.activation(out=qraw[:, :W], in_=bank(bi, W), func=AF.Identity), [bk(bi)], [("qraw", i)])
                S.op("dve", lambda: V.tensor_tensor(out=qsq[:, :W], in0=qraw[:, :W], in1=qraw[:, :W], op=ALU.mult), [("qraw", i)], [("qsq", i)])
                yield
                b2 = nextbank(4, 8)
                S.op("pe", lambda: PE.matmul(bank(b2, W), lhsT=bdB, rhs=qsq[:, :W], start=True, stop=True), [("qsq", i), ("cb",)], [bk(b2)])
                S.op("act", lambda: A.activation(out=qln[:, :W], in_=bank(b2, W), func=AF.Ln, scale=1.0 / 64, bias=epsc[:]), [bk(b2), ("epsc",)], [("qln", i)])
                S.op("act", lambda: A.activation(out=qln[:, :W], in_=qln[:, :W], func=AF.Exp, scale=-0.5), [("qln", i)], [("qln", i)])
                if isc:
                    S.op("dve", lambda: V.scalar_tensor_tensor(out=dst[:, h, c0:c0 + W], in0=qraw[:, :W], scalar=gcol, in1=qln[:, :W], op0=ALU.mult, op1=ALU.mult),
                         [("qraw", i), ("qln", i), gkey], [(dkey, h, ti)])
                    return
                S.op("dve", lambda: V.scalar_tensor_tensor(out=qn[:, :W], in0=qraw[:, :W], scalar=gcol, in1=qln[:, :W], op0=ALU.mult, op1=ALU.mult),
                     [("qraw", i), ("qln", i), gkey], [("qn", i)])
                yield
                b3 = nextbank(4, 8)
                S.op("pe", lambda: PE.matmul(bank(b3, W), lhsT=rotB, rhs=qn[:, :W], start=True, stop=True), [("qn", i), ("cb",)], [bk(b3)])
                S.op("pool", lambda: G.tensor_tensor(out=t1[:, :W], in0=qn[:, :W], in1=cosT[:, c0:c0 + W], op=ALU.mult), [("qn", i), ("rope",)], [("qln", i)])
                S.op("dve", lambda: V.tensor_tensor(out=t2[:, :W], in0=bank(b3, W), in1=sinT[:, c0:c0 + W], op=ALU.mult), [bk(b3), ("rope",)], [("qraw", i)])
                yield
                eng, Eh = ("pool", G) if n_ % 2 == 0 else ("dve", V)
                S.op(eng, lambda: Eh.tensor_tensor(out=dst[:, h, c0:c0 + W], in0=t1[:, :W], in1=t2[:, :W], op=ALU.add), [("qln", i), ("qraw", i)], [(dkey, h, ti)])

            for g in range(2):
                wb, wk = load_wgroup(g * 512)
                for cbk in range(4):
                    h = g * 4 + cbk
                    for ti, (c0, W, isc) in enumerate(all_tiles):
                        bi = proj_block(wb, wk, cbk, c0, W, ti)
                        tick(qk_evac(bi, KT, "KT", kg[:, l:l + 1], ("kg",), h, c0, W, ti, isc))
            for g in range(2):
                wb, wk = load_wgroup(1024 + g * 512)
                for t in range(18):
                    ti = t // 4 if t < 16 else 4
                    bi = nextbank(0, 4)
                    for kc in range(8):
                        S.op("pe", lambda kc=kc, t=t, bi=bi, wb=wb: PE.matmul(bank(bi), lhsT=hT[:, kc, t * 128:(t + 1) * 128], rhs=wb[:, kc, :], start=(kc == 0), stop=(kc == 7)),
                             [wk, ("hT", kc, ti)], [bk(bi)])
                    tick()
                    copy_any(Vt[:, t, g * 512:(g + 1) * 512], bank(bi), [bk(bi)], [("Vt", t, g)])
            for g in range(2):
                wb, wk = load_wgroup(2048 + g * 512)
                for cbk in range(4):
                    h = g * 4 + cbk
                    for ti, (c0, W, isc) in enumerate(out_tiles):
                        bi = proj_block(wb, wk, cbk, c0, W, ti)
                        tick(qk_evac(bi, QT, "QT", qg[:, l:l + 1], ("qg",), h, c0, W, ti, isc))
            for gg in range(4):
                wb, wk = load_wgroup(3584 + gg * 512)
                for cbk in range(4):
                    ch = gg * 4 + cbk
                    for ti, (c0, W, isc) in enumerate(out_tiles):
                        bi = proj_block(wb, wk, cbk, c0, W, ti)
                        tick()
                        i = qst["n"] % NSET
                        qst["n"] += 1
                        gs = gst_b[i]
                        S.op("act", lambda gs=gs, bi=bi, W=W: A.activation(out=gs[:, :W], in_=bank(bi, W), func=AF.Sigmoid), [bk(bi)], [("qsq", i)])
                        S.dma("sp", lambda gs=gs, ch=ch, c0=c0, W=W: nc.sync.dma_start(out=gates[ti][:, ch * 512:ch * 512 + W], in_=gs[:, :W]), [("qsq", i)], [("gates", ch, ti)])
            while pending:
                tick()
            S.free("hT", "wb", "qraw", "qsq", "qln", "qn")

            oT = salloc("oT", [128, 8, TA], BF16, at=R0)
            PT_b = [salloc("PT%d" % i, [128, 1024], BF16, at=R4 + i * 2048) for i in range(3)]
            AT = R4 + 6144
            at_b = [[salloc("at%d_%d" % (i, k), [128, 512], BF16 if k == 5 else F32, at=AT + (i * 8 + k) * 2048) for k in range(8)] for i in range(2)]
            q_tiles = [(ti, c0, W, isc) for ti, (c0, W, isc) in enumerate(out_tiles)]
            items = []
            for h in range(NH):
                for (ti, c0, W, isc) in q_tiles:
                    kts = [16, 17] if isc else list(range(18))
                    for ki, kt in enumerate(kts):
                        items.append((h, ti, c0, W, isc, kt, ki == 0, ki == len(kts) - 1))

            def s_stage(n):
                h, ti, c0, W, isc, kt, first, last = items[n]
                kti = kt // 4 if kt < 16 else 4
                sa = (n % 2) * 2
                pi = n % 3
                PT = PT_b[pi]
                S.op("pe", lambda: PE.matmul(bank(sa, W), lhsT=KT[0:64, h, kt * 128:(kt + 1) * 128], rhs=QT[0:64, h, c0:c0 + W], start=True, stop=True),
                     [("KT", h, kti), ("QT", h, ti)], [bk(sa)])
                S.op("pe", lambda: PE.matmul(bank(sa + 1, W), lhsT=KT[64:128, h, kt * 128:(kt + 1) * 128], rhs=QT[64:128, h, c0:c0 + W], start=True, stop=True),
                     [("KT", h, kti), ("QT", h, ti)], [bk(sa + 1)])
                S.op("act", lambda: A.activation(out=PT[:].rearrange("p (i w) -> p i w", i=2)[:, :, :W],
                                                 in_=ps[:, sa * 512:(sa + 2) * 512].rearrange("p (i w) -> p i w", i=2)[:, :, :W], func=AF.Exp, scale=0.125),
                     [bk(sa), bk(sa + 1)], [("PT", pi)])

            grp = {"g": -1}

            def av1_stage(n):
                h, ti, c0, W, isc, kt, first, last = items[n]
                pi = n % 3
                PT = PT_b[pi]
                if first:
                    grp["g"] += 1
                qi = grp["g"] % 2
                acc1 = at_b[qi][4]
                S.op("pe", lambda: PE.matmul(bank(4, W), lhsT=Vt[:, kt, h * 128:(h + 1) * 128], rhs=PT[:, 0:W], start=first, stop=last),
                     [("Vt", kt, h // 4), ("PT", pi)], [bk(4)])
                if first:
                    S.op("dve", lambda: V.tensor_copy(out=acc1[:, :W], in_=PT[:, 0:W]), [("PT", pi)], [("at", qi, 4)])
                else:
                    S.op("dve", lambda: V.tensor_tensor(out=acc1[:, :W], in0=acc1[:, :W], in1=PT[:, 0:W], op=ALU.add), [("PT", pi), ("at", qi, 4)], [("at", qi, 4)])

            def av2_stage(n):
                h, ti, c0, W, isc, kt, first, last = items[n]
                pi = n % 3
                PT = PT_b[pi]
                S.op("pe", lambda: PE.matmul(bank(6, W), lhsT=Vt[:, kt, h * 128:(h + 1) * 128], rhs=PT[:, 512:512 + W], start=first, stop=last),
                     [("Vt", kt, h // 4), ("PT", pi)], [bk(6)])
                S.op("pe", lambda: PE.matmul(bank(7, W), lhsT=onesB, rhs=PT[:, 512:512 + W], start=first, stop=last),
                     [("cb",), ("PT", pi)], [bk(7)])

            def post1(n):
                h, ti, c0, W, isc, kt, first, last = items[n]
                qi = grp["g"] % 2
                l1, l2, o1, o2, acc1, acc1b = at_b[qi][0:6]
                kk = lambda k: ("at", qi, k)
                S.op("dve", lambda: V.tensor_copy(out=o1[:, :W], in_=bank(4, W)), [bk(4)], [kk(2)])
                S.op("act", lambda: A.activation(out=l2[:, :W], in_=bank(7, W), func=AF.Ln), [bk(7)], [kk(1)])
                S.op("dve", lambda: V.tensor_copy(out=o2[:, :W], in_=bank(6, W)), [bk(6)], [kk(3)])
                S.op("dve", lambda: V.tensor_copy(out=acc1b[:, :W], in_=acc1[:, :W]), [kk(4)], [kk(5)])
                S.op("act", lambda: A.activation(out=l2[:, :W], in_=l2[:, :W], func=AF.Exp, scale=-1.0), [kk(1)], [kk(1)])
                return (h, ti, c0, W, qi)

            def post2(info):
                h, ti, c0, W, qi = info
                l1, l2, o1, o2, acc1, acc1b = at_b[qi][0:6]
                kk = lambda k: ("at", qi, k)
                S.op("pe", lambda: PE.matmul(bank(5, W), lhsT=onesB, rhs=acc1b[:, :W], start=True, stop=True), [kk(5), ("cb",)], [bk(5)])
                S.op("act", lambda: A.activation(out=l1[:, :W], in_=bank(5, W), func=AF.Ln), [bk(5)], [kk(0)])
                S.op("act", lambda: A.activation(out=l1[:, :W], in_=l1[:, :W], func=AF.Exp, scale=-1.0), [kk(0)], [kk(0)])
                S.op("dve", lambda: V.tensor_tensor(out=o1[:, :W], in0=o1[:, :W], in1=l1[:, :W], op=ALU.mult), [kk(2), kk(0)], [kk(2)])
                S.op("dve", lambda: V.scalar_tensor_tensor(out=o2[:, :W], in0=o2[:, :W], scalar=neglam[:, l:l + 1], in1=l2[:, :W], op0=ALU.mult, op1=ALU.mult),
                     [kk(3), kk(1), ("neglam",)], [kk(3)])
                S.op("pool", lambda: G.tensor_tensor(out=oT[:, h, c0:c0 + W], in0=o2[:, :W], in1=o1[:, :W], op=ALU.add), [kk(2), kk(3)], [("oT", h, ti)])

            wpa = salloc("wpa", [128, 8, D], BF16, at=R1)
            wpf = salloc("wpf", [128, 4, D], BF16, at=R1 + 18432)
            wo = salloc("wo", [128, 8, D], BF16, at=R2)
            wpa_v = wpa_d[l].rearrange("(kc p) c -> p kc c", p=128)
            wpf_v = wpf_d[l].rearrange("(kc p) c -> p kc c", p=128)
            wo_v = wo_d[l].rearrange("(kc p) c -> p kc c", p=128)
            last_ti = q_tiles[-1][0]

            def dead(nm, heads):
                return [(nm, h_, t_) for h_ in heads for t_ in range(5)]

            s_stage(0)
            s_stage(1)
            pend2 = []
            for n in range(len(items)):
                av1_stage(n)
                if n + 2 < len(items):
                    s_stage(n + 2)
                av2_stage(n)
                while pend2:
                    post2(pend2.pop(0))
                if items[n][7]:
                    pend2.append(post1(n))
                    if items[n][1] == last_ti and items[n][0] == 3:
                        for hf in range(2):
                            S.dma("pool", lambda hf=hf: nc.gpsimd.dma_start(out=wpa[:, :, hf * 512:(hf + 1) * 512], in_=wpa_v[:, :, hf * 512:(hf + 1) * 512]), [], [("wpa", hf)] + dead("KT", range(4)))
                        for hf in range(2):
                            S.dma("pool", lambda hf=hf: nc.gpsimd.dma_start(out=wo[:, :, hf * 512:(hf + 1) * 512], in_=wo_v[:, :, hf * 512:(hf + 1) * 512]), [], [("wo", hf)] + dead("QT", range(4)))
                    if items[n][1] == last_ti and items[n][0] == 5:
                        for hf in range(2):
                            S.dma("pool", lambda hf=hf: nc.gpsimd.dma_start(out=wpf[:, :, hf * 512:(hf + 1) * 512], in_=wpf_v[:, :, hf * 512:(hf + 1) * 512]), [], [("wpf", hf)] + dead("KT", [4, 5]))
            while pend2:
                post2(pend2.pop(0))
            S.free("at", "PT", "KT", "QT", "Vt")
            M0 = R1 + 53248
            gat_b = [salloc("gat%d" % i, [128, 16, 512], BF16, at=M0 + i * 16384) for i in range(2)]
            mx_b = [salloc("mx%d" % i, [128, 8, 512], F32, at=M0 + 32768 + i * 16384) for i in range(2)]
            fot_b = [salloc("fot%d" % i, [128, 4, 512], BF16, at=M0 + 65536 + i * 4096) for i in range(2)]
            uT_b = [salloc("uT%d" % i, [128, 8, 512], BF16, at=M0 + 73728 + i * 8192) for i in range(2)]
            u_b = [salloc("u%d" % i, [128, 512], BF16, at=M0 + 90112 + i * 1024) for i in range(4)]

            def p5_load(tix):
                c0, W, isc = out_tiles[tix]
                i = tix % 2
                gat, mx, fot = gat_b[i], mx_b[i], fot_b[i]
                S.dma("sp", lambda: nc.sync.dma_start(out=gat[:, :, :W], in_=gates[tix].rearrange("p (c t) -> p c t", c=16)[:, :, :W]),
                      [("gates", ch, tix) for ch in range(16)], [("gat", i)])
                S.dma("sp", lambda: nc.sync.dma_start(out=fot[:, :, :W], in_=fos[tix].rearrange("p (j t) -> p j t", j=4)[:, :, :W]), [("fos", tix)], [("fot", i)])
                S.dma("sp", lambda: nc.sync.dma_start(out=mx[:, :, :W], in_=xs_cols(c0, W)), [("xs", tix)], [("mx", i)])

            p5_load(0)
            SL0 = R1 + 124928
            osq_b = [salloc("osq%d" % i, [128, 4, 512], BF16, at=SL0 + i * 13312) for i in range(2)]
            oln_b = [salloc("oln%d" % i, [128, 4, 512], F32, at=SL0 + i * 13312 + 4096) for i in range(2)]
            units = [[(h, ti, c0, W) for (ti, c0, W, isc) in q_tiles if not isc] for h in range(NH)]
            if ctx_out:
                units += [[(h, 4, T, TC) for h in range(4)], [(h, 4, T, TC) for h in range(4, 8)]]
            def sl_a(ui):
                unit = units[ui]
                si = ui % 2
                osq, oln = osq_b[si], oln_b[si]
                ne = len(unit)
                Wm = unit[0][3]
                for e, (h, ti, c0, W) in enumerate(unit):
                    eng, Eh = ("pool", G) if e % 2 == 0 else ("dve", V)
                    S.op(eng, lambda e=e, h=h, c0=c0, W=W, Eh=Eh: Eh.tensor_tensor(out=osq[:, e, :W], in0=oT[:, h, c0:c0 + W], in1=oT[:, h, c0:c0 + W], op=ALU.mult), [("oT", h, ti)], [("osq", si, e)])
                    S.op("pe", lambda e=e, W=W: PE.matmul(bank(si * 4 + e, W), lhsT=onesB, rhs=osq[:, e, :W], start=True, stop=True), [("osq", si, e), ("cb",)], [bk(si * 4 + e)])
                S.op("act", lambda: A.activation(out=oln[:, :ne, :Wm], in_=ps[:, si * 2048:(si + 1) * 2048].rearrange("p (e w) -> p e w", e=4)[:, :ne, :Wm], func=AF.Ln, scale=1.0 / 128, bias=epsc[:]),
                     [bk(si * 4 + e) for e in range(ne)] + [("epsc",)], [("oln", si)])
                S.op("act", lambda: A.activation(out=oln[:, :ne, :Wm], in_=oln[:, :ne, :Wm], func=AF.Exp, scale=-0.5), [("oln", si)], [("oln", si)])

            def sl_b(ui):
                unit = units[ui]
                si = ui % 2
                oln = oln_b[si]
                for e, (h, ti, c0, W) in enumerate(unit):
                    S.op("dve", lambda e=e, h=h, c0=c0, W=W: V.scalar_tensor_tensor(out=oT[:, h, c0:c0 + W], in0=oT[:, h, c0:c0 + W], scalar=gsub[:, l:l + 1], in1=oln[:, e, :W], op0=ALU.mult, op1=ALU.mult),
                         [("oT", h, ti), ("oln", si), ("gsub",)], [("oT", h, ti)])

            sl_a(0)
            for ui in range(len(units)):
                if ui + 1 < len(units):
                    sl_a(ui + 1)
                sl_b(ui)
            S.free("osq", "oln")

            F0 = R1 + NJ * TA * 2
            wgu_b = [salloc("wgu0", [128, 8, 512], BF16, at=R1 + 147456), salloc("wgu1", [128, 8, 512], BF16, at=F0)]
            wd_b = [salloc("wdn%d" % i, [128, NJ, 256], BF16, at=F0 + 8192 + i * 11264) for i in range(2)]
            sg_b = [salloc("sgl%d" % i, [128, 512], F32, at=F0 + 30720 + i * 2048) for i in range(2)]
            fx_b = [salloc("fx%d" % i, [128, 512], F32, at=F0 + 34816 + i * 2048) for i in range(3)]
            assert F0 + 34816 + 3 * 2048 <= R1 + 147456 and R1 + 147456 + 8192 <= SB_END
            wgu_v = wgu_d[l].rearrange("(kc p) c -> p kc c", p=128)
            wd_v = wd_d[l].rearrange("(j p) c -> p j c", p=128)

            def load_wgu(jj):
                wi = jj % 2
                wg = wgu_b[wi]
                S.dma("pool", lambda: nc.gpsimd.dma_start(out=wg[:, :, 0:256], in_=wgu_v[:, :, jj * 256:(jj + 1) * 256]), [], [("wgu", wi, 0)])
                S.dma("pool", lambda: nc.gpsimd.dma_start(out=wg[:, :, 256:512], in_=wgu_v[:, :, DFF + jj * 256:DFF + (jj + 1) * 256]), [], [("wgu", wi, 1)])

            def load_wd(mp):
                wi = mp % 2
                wdn = wd_b[wi]
                S.dma("pool", lambda: nc.gpsimd.dma_start(out=wdn[:, 0:11, :], in_=wd_v[:, 0:11, mp * 256:(mp + 1) * 256]), [], [("wdn", wi, 0)])
                S.dma("pool", lambda: nc.gpsimd.dma_start(out=wdn[:, 11:22, :], in_=wd_v[:, 11:22, mp * 256:(mp + 1) * 256]), [], [("wdn", wi, 1)])

            load_wgu(0)
            h2T = salloc("h2T", [128, 8, TA], BF16, at=R0)
            nsq2 = salloc("nsq2", [128, 8, 512], BF16, at=R1 + 26624)
            nln2 = salloc("nln2", [128, 512], F32, at=R1 + 26624 + 8192)

            ust = {"n": 0}
            for tix, (c0, W, isc) in enumerate(out_tiles):
                who = 2 if isc else b
                i = tix % 2
                gat, mx, uT, fot = gat_b[i], mx_b[i], uT_b[i], fot_b[i]
                if tix + 1 < len(out_tiles):
                    p5_load(tix + 1)
                for mc in range(8):
                    ba = nextbank(0, 8)
                    for kc in range(8):
                        S.op("pe", lambda kc=kc, mc=mc, ba=ba: PE.matmul(bank(ba, W), lhsT=wpa[:, kc, mc * 128:(mc + 1) * 128], rhs=oT[:, kc, c0:c0 + W], start=(kc == 0), stop=(kc == 7)),
                             [("wpa", mc // 4), ("oT", kc, tix)], [bk(ba)])
                    bf = nextbank(0, 8)
                    for kc in range(4):
                        S.op("pe", lambda kc=kc, mc=mc, bf=bf: PE.matmul(bank(bf, W), lhsT=wpf[:, kc, mc * 128:(mc + 1) * 128], rhs=fot[:, kc, :W], start=(kc == 0), stop=(kc == 3)),
                             [("wpf", mc // 4), ("fot", i)], [bk(bf)])
                    ui = ust["n"] % 2
                    ust["n"] += 1
                    u1, u2 = u_b[2 * ui], u_b[2 * ui + 1]
                    S.op("dve", lambda u1=u1, ba=ba, mc=mc: V.tensor_tensor(out=u1[:, :W], in0=bank(ba, W), in1=gat[:, mc, :W], op=ALU.mult), [bk(ba), ("gat", i)], [("u", 2 * ui)])
                    S.op("dve", lambda u2=u2, bf=bf, mc=mc: V.tensor_tensor(out=u2[:, :W], in0=bank(bf, W), in1=gat[:, 8 + mc, :W], op=ALU.mult), [bk(bf), ("gat", i)], [("u", 2 * ui + 1)])
                    S.op("pool", lambda u1=u1, u2=u2, mc=mc: G.tensor_tensor(out=uT[:, mc, :W], in0=u1[:, :W], in1=u2[:, :W], op=ALU.add), [("u", 2 * ui), ("u", 2 * ui + 1)], [("uT", i, mc)])
                for mc in range(8):
                    bz = nextbank(0, 8)
                    for kc in range(8):
                        S.op("pe", lambda kc=kc, mc=mc, bz=bz: PE.matmul(bank(bz, W), lhsT=wo[:, kc, mc * 128:(mc + 1) * 128], rhs=uT[:, kc, :W], start=(kc == 0), stop=(kc == 7)),
                             [("wo", mc // 4), ("uT", i, kc)], [bk(bz)])
                    S.op("dve", lambda mc=mc, bz=bz: V.scalar_tensor_tensor(out=mx[:, mc, :W], in0=bank(bz, W), scalar=mcol(l, 2, mc, who), in1=mx[:, mc, :W], op0=ALU.mult, op1=ALU.add),
                         [bk(bz), ("mx", i), ("modT",)], [("mx", i)])
                S.dma("sp", lambda mx=mx, c0=c0, W=W: nc.sync.dma_start(out=xs_cols(c0, W), in_=mx[:, :, :W]), [("mx", i)], [("xs", tix)])
                S.op("pool", lambda mx=mx, W=W: G.tensor_tensor(out=nsq2[:, 0:4, :W], in0=mx[:, 0:4, :W], in1=mx[:, 0:4, :W], op=ALU.mult), [("mx", i)], [("nsq2", 0)])
                S.op("dve", lambda mx=mx, W=W: V.tensor_tensor(out=nsq2[:, 4:8, :W], in0=mx[:, 4:8, :W], in1=mx[:, 4:8, :W], op=ALU.mult), [("mx", i)], [("nsq2", 1)])
                bn = nextbank(0, 8)
                for c in range(8):
                    S.op("pe", lambda c=c, bn=bn, W=W: PE.matmul(bank(bn, W), lhsT=onesB, rhs=nsq2[:, c, :W], start=(c == 0), stop=(c == 7)), [("nsq2", c // 4), ("cb",)], [bk(bn)])
                S.op("act", lambda bn=bn, W=W: A.activation(out=nln2[:, :W], in_=bank(bn, W), func=AF.Ln, scale=1.0 / D, bias=epsc[:]), [bk(bn), ("epsc",)], [("nln2",)])
                S.op("act", lambda W=W: A.activation(out=nln2[:, :W], in_=nln2[:, :W], func=AF.Exp, scale=-0.5), [("nln2",)], [("nln2",)])
                S.op("dve", lambda mx=mx, W=W: V.tensor_tensor(out=mx[:, :, :W], in0=mx[:, :, :W], in1=nln2[:, :W].unsqueeze(1).broadcast_to([128, 8, W]), op=ALU.mult), [("mx", i), ("nln2",)], [("mx", i)])
                for c in range(8):
                    S.op("act", lambda mx=mx, c=c, c0=c0, W=W, who=who: A.activation(out=h2T[:, c, c0:c0 + W], in_=mx[:, c, :W], func=AF.Identity,
                                                                                     scale=Gm[:, l, who, 1, c:c + 1], bias=mcol(l, 3, c, who)),
                         [("mx", i), ("Gm",), ("modT",)], [("hT", c, tix), ("oT", c, tix)])
            hT = h2T
            S.free("oT", "wpa", "wpf", "wo", "gat", "mx", "uT", "fot", "u", "nsq2", "nln2")

            actT = salloc("actT", [128, NJ, TA], BF16, at=R1)
            fst = {"s": 0, "x": 0}
            for jj in range(NJ // 2):
                wi = jj % 2
                wg = wgu_b[wi]
                if jj + 1 < NJ // 2:
                    load_wgu(jj + 1)
                if jj == 6:
                    load_wd(0)
                if jj == 9:
                    load_wd(1)
                for jl in range(2):
                    j = jj * 2 + jl
                    for ti, (c0, W, isc) in enumerate(out_tiles):
                        bg = nextbank(0, 8)
                        for kc in range(8):
                            S.op("pe", lambda kc=kc, jl=jl, bg=bg, wg=wg, c0=c0, W=W: PE.matmul(bank(bg, W), lhsT=wg[:, kc, jl * 128:(jl + 1) * 128], rhs=hT[:, kc, c0:c0 + W], start=(kc == 0), stop=(kc == 7)),
                                 [("wgu", wi, 0), ("hT", kc, ti)], [bk(bg)])
                        bu = nextbank(0, 8)
                        for kc in range(8):
                            S.op("pe", lambda kc=kc, jl=jl, bu=bu, wg=wg, c0=c0, W=W: PE.matmul(bank(bu, W), lhsT=wg[:, kc, 256 + jl * 128:256 + (jl + 1) * 128], rhs=hT[:, kc, c0:c0 + W], start=(kc == 0), stop=(kc == 7)),
                                 [("wgu", wi, 1), ("hT", kc, ti)], [bk(bu)])
                        si = fst["s"] % 2
                        fst["s"] += 1
                        sgl = sg_b[si]
                        S.op("act", lambda sgl=sgl, bg=bg, W=W: A.activation(out=sgl[:, :W], in_=bank(bg, W), func=AF.Silu), [bk(bg)], [("sgl", si)])
                        S.op("dve", lambda sgl=sgl, bu=bu, j=j, c0=c0, W=W: V.tensor_tensor(out=actT[:, j, c0:c0 + W], in0=bank(bu, W), in1=sgl[:, :W], op=ALU.mult), [bk(bu), ("sgl", si)], [("actT", j, ti)])
            for mp in range(4):
                wi = mp % 2
                wdn = wd_b[wi]
                if mp >= 1 and mp + 1 < 4:
                    load_wd(mp + 1)
                for ml in range(2):
                    mc = mp * 2 + ml
                    for ti, (c0, W, isc) in enumerate(out_tiles):
                        who = 2 if isc else b
                        xi = fst["x"] % 3
                        fst["x"] += 1
                        fx = fx_b[xi]
                        S.dma("sp", lambda fx=fx, mc=mc, c0=c0, W=W: nc.sync.dma_start(out=fx[:, :W], in_=xs_chunk(mc, ti, W)), [("xs", ti)], [("fx", xi)])
                        bz = nextbank(0, 8)
                        for j in range(NJ):
                            S.op("pe", lambda j=j, ml=ml, bz=bz, wdn=wdn, c0=c0, W=W: PE.matmul(bank(bz, W), lhsT=wdn[:, j, ml * 128:(ml + 1) * 128], rhs=actT[:, j, c0:c0 + W], start=(j == 0), stop=(j == NJ - 1)),
                                 [("wdn", wi, j // 11), ("actT", j, ti)], [bk(bz)])
                        S.op("dve", lambda fx=fx, bz=bz, mc=mc, W=W, who=who: V.scalar_tensor_tensor(out=fx[:, :W], in0=bank(bz, W), scalar=mcol(l, 5, mc, who), in1=fx[:, :W], op0=ALU.mult, op1=ALU.add),
                             [bk(bz), ("fx", xi), ("modT",)], [("fx", xi)])
                        S.dma("sp", lambda fx=fx, mc=mc, c0=c0, W=W: nc.sync.dma_start(out=xs_chunk(mc, ti, W), in_=fx[:, :W]), [("fx", xi)], [("xs", ti)])
            S.free("hT", "actT", "wgu", "wdn", "sgl", "fx")

        xtt_b = [salloc("oxt%d" % i, [128, 8, 512], F32, at=R0 + i * 16384) for i in range(2)]
        xo_b = [salloc("oxo%d" % i, [128, 4, D], F32, at=R0 + 32768 + i * 16384) for i in range(2)]

        def p8_load(g):
            xtt = xtt_b[g % 2]
            S.dma("sp", lambda: nc.sync.dma_start(out=xtt[:], in_=xs_cols(g * 512, 512)), [("xs", g)], [("oxt", g % 2)])

        p8_load(0)
        p8_load(1)
        for g in range(4):
            xtt = xtt_b[g % 2]
            xo = xo_b[g % 2]
            for t in range(4):
                pb = 2 * t
                for c in range(8):
                    S.op("pe", lambda t=t, c=c, pb=pb: PE.transpose(out=ps[:, pb * 512 + c * 128:pb * 512 + (c + 1) * 128], in_=xtt[:, c, t * 128:(t + 1) * 128], identity=identF[:]),
                         [("oxt", g % 2), ("identF",)], [bk(pb + c // 4)])
                copy_any(xo[:, t, :], ps[:, pb * 512:pb * 512 + 1024], [bk(pb), bk(pb + 1)], [("oxo", g % 2, t)])
            if g + 2 < 4:
                p8_load(g + 2)
            S.dma("sp", lambda: nc.sync.dma_start(out=out_d[b, g * 512:(g + 1) * 512, :].rearrange("(t p) d -> p t d", p=128), in_=xo[:]),
                  [("oxo", g % 2, t) for t in range(4)], [("out", b, g)])
        S.free("oxt", "oxo")

    S.finish()
    return nc, S


_CONST = {}


def _consts():
    if _CONST:
        return _CONST
    bf = ml_dtypes.bfloat16
    ones = np.ones((128, 128), np.float32)
    bd = np.zeros((128, 128), np.float32)
    bd[:64, :64] = 1.0
    bd[64:, 64:] = 1.0
    rot = np.zeros((128, 128), np.float32)
    for m in range(128):
        d = m % 64
        half = (d // 16) % 2
        if half == 0:
            rot[m + 16, m] = -1.0
        else:
            rot[m - 16, m] = 1.0
    cidx = np.arange(128)
    angc = 2.0 * np.pi * ((cidx[:, None] * cidx[None, :]) % 128) / 128.0
    Cc = np.cos(angc)
    Scn = -np.sin(angc)
    _CONST["cbf"] = np.ascontiguousarray(np.stack([ones, bd, rot, Cc, Scn], axis=1)).astype(bf)
    _CONST["ident"] = np.eye(128, dtype=np.float32)
    t = np.arange(T)
    r = (t // 64).astype(np.float32)
    col = (t % 64).astype(np.float32)
    inv_freq = (10000.0 ** (-np.arange(0, 32, 2, dtype=np.float32) / 32.0)).astype(np.float32)
    ang_r = r[:, None] * inv_freq[None, :]
    ang_c = col[:, None] * inv_freq[None, :]
    ang = np.concatenate([ang_r, ang_r, ang_c, ang_c], axis=-1)
    cosT = np.cos(ang).T
    sinT = np.sin(ang).T
    rope = np.stack([np.concatenate([cosT, cosT], 0), np.concatenate([sinT, sinT], 0)], axis=1)
    _CONST["rope"] = np.ascontiguousarray(rope).astype(bf)
    tt = np.arange(T, dtype=np.int64)
    angt = 2.0 * np.pi * ((tt[:, None] * tt[None, :]) % T).astype(np.float64) / T
    sc = 1.0 / math.sqrt(T * 128.0)
    def _lay(tab):
        a = tab.astype(np.float32).reshape(2, 8, 128, 4, 512).transpose(3, 0, 2, 1, 4)
        return np.ascontiguousarray(a).reshape(4, 2, 128, 8 * 512).astype(bf)
    _CONST["ctab"] = _lay(np.cos(angt) * sc)
    _CONST["stab"] = _lay(np.sin(angt) * sc)
    t2 = np.arange(TC, dtype=np.int64)
    ang2 = 2.0 * np.pi * ((t2[:, None] * t2[None, :]) % TC).astype(np.float64) / TC
    sc2 = 1.0 / math.sqrt(TC * 128.0)
    c2 = (np.cos(ang2) * sc2).reshape(2, 128, TC).transpose(1, 0, 2)
    s2 = (np.sin(ang2) * sc2).reshape(2, 128, TC).transpose(1, 0, 2)
    _CONST["c256"] = np.ascontiguousarray(np.stack([c2, s2], axis=1)).astype(np.float32).astype(bf)
    return _CONST


def _cols(v):
    v = np.asarray(v, np.float32)
    n = v.shape[-1] // 128
    return np.ascontiguousarray(np.moveaxis(v.reshape(v.shape[:-1] + (n, 128)), -1, 0))


_PROG = {}


def _program():
    if "nc" not in _PROG:
        _, S1 = build_program(None)
        rank = {e: {v: i + 1 for i, v in enumerate(sorted(S1.needed[e]))} for e in Sched.ENGS}
        nc, _ = build_program(rank)
        _PROG["nc"] = nc
    return _PROG["nc"]


def make_in_maps(inp, ncores=8):
    C = _consts()
    f = lambda a: np.ascontiguousarray(np.asarray(a, np.float32))
    shared = {
        "w_ada": f(inp["w_ada"]), "w_in": f(inp["w_in"]), "w_proj_attn": f(inp["w_proj_attn"]),
        "w_proj_fourier": f(inp["w_proj_fourier"]), "w_out": f(inp["w_out"]),
        "w_gate_up": f(inp["w_gate_up"]), "w_down": f(inp["w_down"]),
        "bada": _cols(inp["b_ada"]), "n1g": _cols(inp["norm1_g"]), "n2g": _cols(inp["norm2_g"]),
        "qg": np.ascontiguousarray(np.tile(f(inp["q_norm_g"]), (1, 2)).T),
        "kg": np.ascontiguousarray(np.tile(f(inp["k_norm_g"]), (1, 2)).T),
        "sublng": np.ascontiguousarray(f(inp["subln_g"]).T),
        "lamv": np.ascontiguousarray(np.broadcast_to(
            np.stack([f(inp["lambda_q1"]), f(inp["lambda_k1"]), f(inp["lambda_q2"]), f(inp["lambda_k2"])], axis=1)[None], (128, DEPTH, 4, 64))),
        "cbf": C["cbf"], "ident": C["ident"], "rope": C["rope"], "c256": C["c256"], "ctab": C["ctab"], "stab": C["stab"],
    }
    x = f(inp["x"])
    ctx = f(inp["ctx"])
    c = f(inp["c"])
    cc = f(inp["c_ctx"])
    maps = []
    for k in range(ncores):
        cv = np.stack([c[2 * k], c[2 * k + 1], cc, cc], axis=0)
        m = dict(shared)
        m["x2"] = x[2 * k:2 * k + 2]
        m["ctx2"] = ctx[2 * k:2 * k + 2]
        m["cT"] = np.ascontiguousarray(cv.reshape(4, 8, 128).transpose(2, 1, 0))
        maps.append(m)
    return maps


def kernel(**inputs):
    nc = _program()
    maps = make_in_maps(inputs, 8)
    res = run_bass_kernel_spmd(nc, maps, core_ids=list(range(8)))
    return np.concatenate([np.asarray(r["out"], np.float32) for r in res.results], axis=0)
```

```python
import math
import numpy as np
import ml_dtypes
import concourse.bass as bass
import concourse.mybir as mybir
from concourse.bass_utils import run_bass_kernel_spmd

F32 = mybir.dt.float32
BF16 = mybir.dt.bfloat16
AF = mybir.ActivationFunctionType
ALU = mybir.AluOpType
AX = mybir.AxisListType

D = 1024
T = 2048
TC = 256
TA = T + TC
NH = 8
DFF = 2816
NJ = DFF // 128
INW = 5632
EPS = 1e-6
DEPTH = 2
NB = 2
LAM_INIT = [0.8 - 0.6 * math.exp(-0.3 * l) for l in range(DEPTH)]
TILES = [(0, 512, 0), (512, 512, 0), (1024, 512, 0), (1536, 512, 0), (2048, 256, 1)]

SAME_SYNC = True
SB0 = 16512
SB_END = 229344


class Sched:
    EPOCH = 4000
    R = 8
    ENGS = ("pe", "act", "dve", "pool")

    def __init__(self, nc, rank=None):
        self.nc = nc
        self.rank = rank
        self.h = {"pe": nc.tensor, "act": nc.scalar, "dve": nc.vector, "pool": nc.gpsimd, "sp": nc.sync}
        self.idx = {e: 0 for e in self.ENGS}
        self.needed = {e: set() for e in self.ENGS}
        self.waited = {}
        self.state = {}
        self.released = {}
        self.dman = {"sp": 0, "pool": 0, "act": 0}
        self.dcount = {}
        self.esems = {e: [] for e in self.ENGS}
        self.dsems = {}
        self.same_sync = SAME_SYNC
        if rank is not None:
            for e in self.ENGS:
                n = len(rank[e])
                for k in range((n + self.EPOCH - 1) // self.EPOCH + 1):
                    self.esems[e].append(nc.alloc_semaphore("s_%s_%d" % (e, k)))
            for q in ("sp", "pool", "act"):
                for i in range(self.R):
                    self.dsems[(q, i)] = nc.alloc_semaphore("d_%s_%d" % (q, i))

    def _st(self, k):
        st = self.state.get(k)
        if st is None:
            st = [None, dict(self.released)]
            self.state[k] = st
        return st

    def free(self, *names):
        for k in list(self.state.keys()):
            if k[0] in names:
                st = self.state.pop(k)
                if st[0] is not None:
                    p, v = st[0]
                    if self.released.get(p, 0) < v:
                        self.released[p] = v
                for p, v in st[1].items():
                    if self.released.get(p, 0) < v:
                        self.released[p] = v

    def _deps(self, reads, writes):
        deps = {}
        for k in reads:
            st = self._st(k)
            if st[0] is not None:
                p, v = st[0]
                if deps.get(p, 0) < v:
                    deps[p] = v
        for k in writes:
            st = self._st(k)
            if st[0] is not None:
                p, v = st[0]
                if deps.get(p, 0) < v:
                    deps[p] = v
            for p, v in st[1].items():
                if deps.get(p, 0) < v:
                    deps[p] = v
        return deps

    def _wait(self, eng, deps):
        for p, v in deps.items():
            if p == eng and (eng == "pe" or not self.same_sync):
                continue
            key = (eng, p)
            if self.waited.get(key, 0) >= v:
                continue
            self.waited[key] = v
            if isinstance(p, str):
                self.needed[p].add(v)
                if self.rank is not None:
                    r = self.rank[p][v]
                    self.h[eng].wait_ge(self.esems[p][(r - 1) // self.EPOCH], (r - 1) % self.EPOCH + 1)
            else:
                if self.rank is not None:
                    self.h[eng].wait_ge(self.dsems[p], v)

    def _record(self, ev, reads, writes):
        for k in writes:
            st = self._st(k)
            st[0] = ev
            st[1] = {}
        p, v = ev
        for k in reads:
            st = self._st(k)
            if st[1].get(p, 0) < v:
                st[1][p] = v

    def op(self, eng, fn, reads=(), writes=()):
        self._wait(eng, self._deps(reads, writes))
        i = self.idx[eng] + 1
        self.idx[eng] = i
        inst = fn()
        if self.rank is not None:
            r = self.rank[eng].get(i)
            if r is not None:
                inst.then_inc(self.esems[eng][(r - 1) // self.EPOCH], 1)
        self._record((eng, i), reads, writes)

    def dma(self, q, fn, reads=(), writes=()):
        deps = self._deps(reads, writes)
        n = self.dman[q]
        self.dman[q] = n + 1
        prod = (q, n % self.R)
        prev = self.dcount.get(prod, 0)
        if prev > 0 and deps.get(prod, 0) < prev:
            deps[prod] = prev
        self._wait(q, deps)
        cnt = prev + 16
        self.dcount[prod] = cnt
        inst = fn()
        if self.rank is not None:
            inst.then_inc(self.dsems[prod], 16)
        self._record((prod, cnt), reads, writes)

    def finish(self):
        if self.rank is None:
            for e in self.ENGS:
                if self.idx[e] > 0:
                    self.needed[e].add(self.idx[e])
            return
        for prod, cnt in self.dcount.items():
            self.nc.sync.wait_ge(self.dsems[prod], cnt)
        for e in self.ENGS:
            if self.idx[e] > 0:
                r = self.rank[e][self.idx[e]]
                self.nc.sync.wait_ge(self.esems[e][(r - 1) // self.EPOCH], (r - 1) % self.EPOCH + 1)


def build_program(rank=None, nb=NB, depth=DEPTH, dbg=None):
    nc = bass.Bass("TRN2", target_bir_lowering=False)
    S = Sched(nc, rank)

    def din(name, shape, dt=F32):
        return nc.dram_tensor(name, list(shape), dt, kind="ExternalInput").ap()

    x_d = din("x2", [NB, T, D])
    ctx_d = din("ctx2", [NB, TC, D])
    cT_d = din("cT", [128, 8, 4])
    wada_d = din("w_ada", [DEPTH, D, 6 * D])
    bada_d = din("bada", [128, DEPTH, 48])
    n1g_d = din("n1g", [128, DEPTH, 8])
    n2g_d = din("n2g", [128, DEPTH, 8])
    win_d = din("w_in", [DEPTH, D, INW])
    qg_d = din("qg", [128, DEPTH])
    kg_d = din("kg", [128, DEPTH])
    sg_d = din("sublng", [128, DEPTH])
    lamv_d = din("lamv", [128, DEPTH, 4, 64])
    wpa_d = din("w_proj_attn", [DEPTH, D, D])
    wpf_d = din("w_proj_fourier", [DEPTH, 512, D])
    wo_d = din("w_out", [DEPTH, D, D])
    wgu_d = din("w_gate_up", [DEPTH, D, 2 * DFF])
    wd_d = din("w_down", [DEPTH, DFF, D])
    cb_d = din("cbf", [128, 5, 128], BF16)
    ident_d = din("ident", [128, 128])
    rope_d = din("rope", [128, 2, T], BF16)
    c256_d = din("c256", [128, 2, 2, 256], BF16)
    ctab_d = din("ctab", [4, 2, 128, 8 * 512], BF16)
    stab_d = din("stab", [4, 2, 128, 8 * 512], BF16)
    out_d = nc.dram_tensor("out", [NB, T, D], F32, kind="ExternalOutput").ap()
    xs = nc.dram_tensor("xs", [5, 128, 8 * 512], F32, kind="Internal").ap()
    gates = nc.dram_tensor("gates", [5, 128, 16 * 512], BF16, kind="Internal").ap()
    fos = nc.dram_tensor("fos", [5, 128, 4 * 512], BF16, kind="Internal").ap()
    dbg_d = {}
    if dbg:
        for name, shape, dt in dbg:
            dbg_d[name] = nc.dram_tensor("dbg_" + name, list(shape), dt, kind="ExternalOutput").ap()

    cur = [SB0]

    def salloc(name, shape, dt, at=None):
        nbytes = int(np.prod(shape[1:])) * (4 if dt == F32 else 2)
        if at is None:
            off = cur[0]
            cur[0] = (off + nbytes + 31) // 32 * 32
        else:
            off = at
        assert off % 32 == 0 and off + nbytes <= SB_END, (name, off, nbytes)
        return nc.alloc_sbuf_tensor_at(name, list(shape), dt, offset=off)

    identF = salloc("identF", [128, 128], F32)
    cb = salloc("cb", [128, 5, 128], BF16)
    rope = salloc("rope", [128, 2, T], BF16)
    c256 = salloc("c256", [128, 2, 2, 256], BF16)
    cT = salloc("cT", [128, 8, 4], F32)
    scT = salloc("scT", [128, 8, 4], F32)
    bada = salloc("bada", [128, DEPTH, 48], F32)
    n1g = salloc("n1g", [128, DEPTH, 8], F32)
    n2g = salloc("n2g", [128, DEPTH, 8], F32)
    qg = salloc("qg", [128, DEPTH], F32)
    kg = salloc("kg", [128, DEPTH], F32)
    sg = salloc("sg", [128, DEPTH], F32)
    lamv = salloc("lamv", [128, DEPTH, 4, 64], F32)
    modT = salloc("modT", [128, DEPTH, 48, 4], F32)
    Gm = salloc("Gm", [128, DEPTH, 3, 2, 8], F32)
    neglam = salloc("neglam", [128, DEPTH], F32)
    gsub = salloc("gsub", [128, DEPTH], F32)
    epsc = salloc("epsc", [128, 1], F32)
    lamt = salloc("lamt", [128, 8], F32)
    lamp = salloc("lamp", [128, 64], F32)
    A0 = cur[0]
    ARENA = SB_END - A0
    assert ARENA >= 190000, ARENA
    R0 = A0
    R1 = A0 + 36864
    R2 = A0 + 73728
    R3 = A0 + 110592
    R4 = A0 + 147456

    onesB = cb[:, 0, :]
    bdB = cb[:, 1, :]
    rotB = cb[:, 2, :]
    CcB = cb[:, 3, :]
    ScnB = cb[:, 4, :]
    cosT = rope[:, 0, :]
    sinT = rope[:, 1, :]

    ps = nc.alloc_psum_tensor("ps", [128, 4096], F32)

    def bank(i, w=512):
        return ps[:, i * 512:i * 512 + w]

    def bk(i):
        return ("ps", i)

    V = nc.vector
    A = nc.scalar
    G = nc.gpsimd
    PE = nc.tensor

    cnt = {"b": 0, "cp": 0}

    def nextbank(lo, hi):
        n = hi - lo
        cnt["b"] += 1
        return lo + cnt["b"] % n

    def copy_any(out, in_, reads, writes):
        cnt["cp"] += 1
        if cnt["cp"] % 2:
            S.op("act", lambda: A.activation(out=out, in_=in_, func=AF.Identity), reads, writes)
        else:
            S.op("dve", lambda: V.tensor_copy(out=out, in_=in_), reads, writes)

    def ld(q, out, in_, wkey, rkeys=()):
        S.dma(q, lambda: (nc.sync if q == "sp" else nc.gpsimd).dma_start(out=out, in_=in_), rkeys, [wkey])

    ld("sp", identF[:], ident_d, ("identF",))
    ld("sp", cb[:], cb_d, ("cb",))
    ld("sp", rope[:], rope_d, ("rope",))
    ld("sp", c256[:], c256_d, ("c256",))
    ld("sp", cT[:], cT_d, ("cT",))
    ld("sp", bada[:], bada_d, ("bada",))
    ld("sp", n1g[:], n1g_d, ("n1g",))
    ld("sp", n2g[:], n2g_d, ("n2g",))
    ld("sp", qg[:], qg_d, ("qg",))
    ld("sp", kg[:], kg_d, ("kg",))
    ld("sp", sg[:], sg_d, ("sg",))
    ld("sp", lamv[:], lamv_d, ("lamv",))
    S.op("dve", lambda: V.memset(epsc[:], EPS), [], [("epsc",)])
    S.op("act", lambda: A.activation(out=scT[:], in_=cT[:], func=AF.Silu), [("cT",)], [("scT",)])

    wa_bufs = [salloc("wa%d" % i, [128, 8, 512], BF16, at=R0 + i * 8192) for i in range(3)]
    scTb = salloc("scTb", [128, 8, 4], BF16, at=R0 + 3 * 8192)
    S.op("dve", lambda: V.tensor_copy(out=scTb[:], in_=scT[:]), [("scT",)], [("scTb",)])
    wada_v = [wada_d[l].rearrange("(kc p) c -> p kc c", p=128) for l in range(DEPTH)]
    gi = 0
    for l in range(depth):
        for jg in range(12):
            wa = wa_bufs[gi % 3]
            wk = ("wa", gi % 3)
            gi += 1
            S.dma("pool", lambda wa=wa, l=l, jg=jg: nc.gpsimd.dma_start(out=wa[:], in_=wada_v[l][:, :, jg * 512:(jg + 1) * 512]), [], [wk])
            for jb in range(4):
                j = jg * 4 + jb
                bi = nextbank(0, 8)
                for kc in range(8):
                    S.op("pe", lambda wa=wa, jb=jb, kc=kc, bi=bi: PE.matmul(bank(bi, 4), lhsT=wa[:, kc, jb * 128:(jb + 1) * 128], rhs=scTb[:, kc, :], start=(kc == 0), stop=(kc == 7)),
                         [wk, ("scTb",)], [bk(bi)])
                S.op("dve", lambda l=l, j=j, bi=bi: V.tensor_scalar(out=modT[:, l, j, :], in0=bank(bi, 4), scalar1=bada[:, l, j:j + 1], scalar2=None, op0=ALU.add),
                     [bk(bi), ("bada",)], [("modT",)])
        for who in range(3):
            S.op("dve", lambda l=l, who=who: V.scalar_tensor_tensor(out=Gm[:, l, who, 0, :], in0=modT[:, l, 8:16, who], scalar=1.0, in1=n1g[:, l, :], op0=ALU.add, op1=ALU.mult),
                 [("modT",), ("n1g",)], [("Gm",)])
            S.op("dve", lambda l=l, who=who: V.scalar_tensor_tensor(out=Gm[:, l, who, 1, :], in0=modT[:, l, 32:40, who], scalar=1.0, in1=n2g[:, l, :], op0=ALU.add, op1=ALU.mult),
                 [("modT",), ("n2g",)], [("Gm",)])
        for i in range(2):
            S.op("dve", lambda l=l, i=i: V.tensor_tensor(out=lamp[:], in0=lamv[:, l, 2 * i, :], in1=lamv[:, l, 2 * i + 1, :], op=ALU.mult), [("lamv",)], [("lamp",)])
            S.op("dve", lambda i=i: V.reduce_sum(out=lamt[:, i:i + 1], in_=lamp[:], axis=AX.X), [("lamp",)], [("lamt",)])
        S.op("act", lambda: A.activation(out=lamt[:, 2:4], in_=lamt[:, 0:2], func=AF.Exp), [("lamt",)], [("lamt",)])
        S.op("dve", lambda l=l: V.tensor_tensor(out=lamt[:, 4:5], in0=lamt[:, 3:4], in1=lamt[:, 2:3], op=ALU.subtract), [("lamt",)], [("lamt",)])
        S.op("dve", lambda l=l: V.tensor_scalar(out=neglam[:, l:l + 1], in0=lamt[:, 4:5], scalar1=-LAM_INIT[l], scalar2=None, op0=ALU.add), [("lamt",)], [("neglam",)])
        S.op("dve", lambda l=l: V.tensor_scalar(out=gsub[:, l:l + 1], in0=sg[:, l:l + 1], scalar1=(1.0 - LAM_INIT[l]), scalar2=None, op0=ALU.mult), [("sg",)], [("gsub",)])
    S.free("wa", "scTb")

    def mcol(l, part, c, who):
        return modT[:, l, part * 8 + c, who:who + 1]

    def xs_cols(c0, w):
        return xs[c0 // 512].rearrange("p (c t) -> p c t", c=8)[:, :, :w]

    def xs_chunk(mc, ti, w):
        return xs[ti][:, mc * 512:mc * 512 + w]

    def norm_phase(l, b, which, tiles, hT):
        xt_b = [salloc("nx%d" % i, [128, 8, 512], F32, at=R1 + i * 16384) for i in range(3)]
        sq_b = [salloc("nsq%d" % i, [128, 8, 512], BF16, at=R1 + 49152 + i * 8192) for i in range(2)]
        ln_b = [salloc("nln%d" % i, [128, 512], F32, at=R1 + 65536 + i * 2048) for i in range(2)]
        rs_b = [salloc("nrs%d" % i, [128, 512], F32, at=R1 + 69632 + i * 2048) for i in range(2)]
        for ti, (c0, W, isc) in enumerate(tiles):
            who = 2 if isc else b
            xt = xt_b[ti % 3]
            sq = sq_b[ti % 2]
            ln = ln_b[ti % 2]
            rs = rs_b[ti % 2]
            kx, ksq, kln, krs = ("nx", ti % 3), ("nsq", ti % 2), ("nln", ti % 2), ("nrs", ti % 2)
            S.dma("sp", lambda xt=xt, c0=c0, W=W: nc.sync.dma_start(out=xt[:, :, :W], in_=xs_cols(c0, W)), [("xs", ti)], [kx])
            S.op("pool", lambda xt=xt, sq=sq, W=W: G.tensor_tensor(out=sq[:, 0:4, :W], in0=xt[:, 0:4, :W], in1=xt[:, 0:4, :W], op=ALU.mult), [kx], [(ksq[0], ksq[1], 0)])
            S.op("dve", lambda xt=xt, sq=sq, W=W: V.tensor_tensor(out=sq[:, 4:8, :W], in0=xt[:, 4:8, :W], in1=xt[:, 4:8, :W], op=ALU.mult), [kx], [(ksq[0], ksq[1], 1)])
            bi = nextbank(0, 8)
            for c in range(8):
                S.op("pe", lambda sq=sq, c=c, bi=bi, W=W: PE.matmul(bank(bi, W), lhsT=onesB, rhs=sq[:, c, :W], start=(c == 0), stop=(c == 7)), [(ksq[0], ksq[1], c // 4), ("cb",)], [bk(bi)])
            S.op("act", lambda ln=ln, bi=bi, W=W: A.activation(out=ln[:, :W], in_=bank(bi, W), func=AF.Ln, scale=1.0 / D, bias=epsc[:]), [bk(bi), ("epsc",)], [kln])
            S.op("act", lambda ln=ln, rs=rs, W=W: A.activation(out=rs[:, :W], in_=ln[:, :W], func=AF.Exp, scale=-0.5), [kln], [krs])
            S.op("dve", lambda xt=xt, rs=rs, W=W: V.tensor_tensor(out=xt[:, :, :W], in0=xt[:, :, :W], in1=rs[:, :W].unsqueeze(1).broadcast_to([128, 8, W]), op=ALU.mult), [kx, krs], [kx])
            for c in range(8):
                S.op("act", lambda xt=xt, c=c, c0=c0, W=W, who=who: A.activation(out=hT[:, c, c0:c0 + W], in_=xt[:, c, :W], func=AF.Identity,
                                                                                 scale=Gm[:, l, who, which, c:c + 1], bias=mcol(l, 3 * which, c, who)),
                     [kx, ("Gm",), ("modT",)], [("hT", c, ti)])
        S.free("nx", "nsq", "nln", "nrs")

    for b in range(nb):
        xin_b = [salloc("xin%d" % i, [128, 4, D], F32, at=R0 + i * 16384) for i in range(2)]
        xtt_b = [salloc("xtt%d" % i, [128, 8, 512], F32, at=R0 + 32768 + i * 16384) for i in range(2)]

        def p0_load(g):
            c0, W, isc = TILES[g]
            nt = W // 128
            xin = xin_b[g % 2]
            src = (ctx_d[b] if isc else x_d[b, c0:c0 + W, :]).rearrange("(t p) d -> p t d", p=128)
            S.dma("sp", lambda: nc.sync.dma_start(out=xin[:, :nt, :], in_=src), [], [("xin", g % 2)])

        p0_load(0)
        p0_load(1)
        for g, (c0, W, isc) in enumerate(TILES):
            nt = W // 128
            xin = xin_b[g % 2]
            xtt = xtt_b[g % 2]
            for t in range(nt):
                pb = 2 * t
                for j in range(8):
                    S.op("pe", lambda t=t, j=j, pb=pb: PE.transpose(out=ps[:, pb * 512 + j * 128:pb * 512 + (j + 1) * 128], in_=xin[:, t, j * 128:(j + 1) * 128], identity=identF[:]),
                         [("xin", g % 2), ("identF",)], [bk(pb + j // 4)])
                copy_any(xtt[:, :, t * 128:(t + 1) * 128], ps[:, pb * 512:pb * 512 + 1024].rearrange("p (c t) -> p c t", c=8), [bk(pb), bk(pb + 1)], [("xtt", g % 2, t)])
            if g + 2 < len(TILES):
                p0_load(g + 2)
            S.dma("sp", lambda: nc.sync.dma_start(out=xs_cols(c0, W), in_=xtt[:, :, :W]), [("xtt", g % 2, t) for t in range(nt)], [("xs", g)])
        S.free("xin", "xtt")

        for l in range(depth):
            ctx_out = l < DEPTH - 1
            all_tiles = TILES
            lat_tiles = TILES[:4]
            out_tiles = TILES if ctx_out else lat_tiles

            wbufs = [salloc("wb%d" % i, [128, 8, 512], BF16, at=R4 + i * 8192) for i in range(2)]
            win_v = win_d[l].rearrange("(kc p) c -> p kc c", p=128)
            wsched = [3072, 0, 512, 1024, 1536, 2048, 2560, 3584, 4096, 4608, 5120]
            wst = {"issued": 0, "used": 0}

            def issue_wload():
                g_ = wst["issued"]
                if g_ >= len(wsched):
                    return
                wst["issued"] += 1
                i = g_ % 2
                wb = wbufs[i]
                col0 = wsched[g_]
                S.dma("pool", lambda: nc.gpsimd.dma_start(out=wb[:], in_=win_v[:, :, col0:col0 + 512]), [], [("wb", i)])

            def load_wgroup(col0):
                g_ = wst["used"]
                assert wsched[g_] == col0
                wst["used"] += 1
                while wst["issued"] <= g_ + 1:
                    if wst["issued"] >= len(wsched):
                        break
                    issue_wload()
                i = g_ % 2
                return wbufs[i], ("wb", i)

            issue_wload()
            hT = salloc("hT", [128, 8, TA], BF16, at=R0)
            norm_phase(l, b, 0, all_tiles, hT)

            def proj_block(wb, wk, cbk, c0, W, ti, lo=0, hi=4):
                bi = nextbank(lo, hi)
                for kc in range(8):
                    S.op("pe", lambda kc=kc: PE.matmul(bank(bi, W), lhsT=wb[:, kc, cbk * 128:(cbk + 1) * 128], rhs=hT[:, kc, c0:c0 + W], start=(kc == 0), stop=(kc == 7)),
                         [wk, ("hT", kc, ti)], [bk(bi)])
                return bi

            fT = salloc("fT", [128, 4, TA], BF16, at=R3)
            wb, wk = load_wgroup(3072)
            for cbk in range(4):
                for ti, (c0, W, isc) in enumerate(out_tiles):
                    bi = proj_block(wb, wk, cbk, c0, W, ti)
                    copy_any(fT[:, cbk, c0:c0 + W], bank(bi, W), [bk(bi)], [("fT", cbk, ti)])

            AB = salloc("AB", [128, 18, 1024], BF16, at=R1)
            tabs = [[salloc("tab%d%d" % (s_, hf), [128, 8, 512], BF16, at=R2 + (s_ * 2 + hf) * 8192) for hf in range(2)] for s_ in range(2)]
            fost = [salloc("fost%d" % i, [128, 4, 512], BF16, at=R3 + 18432 + i * 4096) for i in range(2)]
            tab_v = [ctab_d, stab_d]

            def load_tabs(tq, hf):
                for s_ in range(2):
                    S.dma("sp", lambda s_=s_: nc.sync.dma_start(out=tabs[s_][hf][:].rearrange("p t q -> p (t q)"), in_=tab_v[s_][tq, hf]), [], [("tab", s_, hf)])

            load_tabs(0, 0)
            load_tabs(0, 1)
            ntt = 18 if ctx_out else 16
            for t in range(ntt):
                ti = t // 4 if t < 16 else 4
                ba = 2 * (t % 4)
                for g in range(4):
                    S.op("pe", lambda t=t, g=g, ba=ba: PE.matmul(ps[:, ba * 512 + g * 128:ba * 512 + (g + 1) * 128], lhsT=fT[:, g, t * 128:(t + 1) * 128], rhs=CcB, start=True, stop=True),
                         [("fT", g, ti), ("cb",)], [bk(ba)])
                for g in range(4):
                    S.op("pe", lambda t=t, g=g, ba=ba: PE.matmul(ps[:, (ba + 1) * 512 + g * 128:(ba + 1) * 512 + (g + 1) * 128], lhsT=fT[:, g, t * 128:(t + 1) * 128], rhs=ScnB, start=True, stop=True),
                         [("fT", g, ti), ("cb",)], [bk(ba + 1)])
                copy_any(AB[:, t, :], ps[:, ba * 512:ba * 512 + 1024], [bk(ba), bk(ba + 1)], [("AB", t)])
            for tq in range(4):
                c0 = tq * 512
                ab = (tq % 2) * 4
                for hf in range(2):
                    for j in range(4):
                        for tl in range(8):
                            t = hf * 8 + tl
                            for s_ in range(2):
                                S.op("pe", lambda j=j, t=t, tl=tl, s_=s_, hf=hf, ab=ab: PE.matmul(bank(ab + j), lhsT=AB[:, t, s_ * 512 + j * 128:s_ * 512 + (j + 1) * 128], rhs=tabs[s_][hf][:, tl, :],
                                                                                          start=(t == 0 and s_ == 0), stop=(t == 15 and s_ == 1)),
                                     [("AB", t), ("tab", s_, hf)], [bk(ab + j)])
                    if tq + 1 < 4:
                        load_tabs(tq + 1, hf)
                fo = fost[tq % 2]
                for j in range(4):
                    copy_any(fo[:, j, :], bank(ab + j), [bk(ab + j)], [("fost", tq % 2)])
                S.dma("sp", lambda fo=fo, c0=c0: nc.sync.dma_start(out=fos[tq].rearrange("p (j t) -> p j t", j=4), in_=fo[:]), [("fost", tq % 2)], [("fos", tq)])
            if ctx_out:
                fo = fost[0]
                for j in range(4):
                    bi = nextbank(0, 8)
                    for tl in range(2):
                        for s_ in range(2):
                            S.op("pe", lambda j=j, tl=tl, s_=s_, bi=bi: PE.matmul(bank(bi, 256), lhsT=AB[:, 16 + tl, s_ * 512 + j * 128:s_ * 512 + (j + 1) * 128], rhs=c256[:, s_, tl, :],
                                                                                  start=(tl == 0 and s_ == 0), stop=(tl == 1 and s_ == 1)),
                                 [("AB", 16 + tl), ("c256",)], [bk(bi)])
                    copy_any(fo[:, j, :256], bank(bi, 256), [bk(bi)], [("fost", 0)])
                S.dma("sp", lambda fo=fo: nc.sync.dma_start(out=fos[4].rearrange("p (j t) -> p j t", j=4)[:, :, :256], in_=fo[:, :, :256]), [("fost", 0)], [("fos", 4)])
            S.free("fT", "AB", "tab", "fost")

            KT = salloc("KT", [128, 8, TA], BF16, at=R1)
            QT = salloc("QT", [128, 8, TA], BF16, at=R2)
            Vt = salloc("Vt", [128, 18, D], BF16, at=R3)
            TB = R4 + 16384
            NSET = 4
            qraw_b = [salloc("qraw%d" % i, [128, 512], F32, at=TB + i * 6144) for i in range(NSET)]
            qsq_b = [salloc("qsq%d" % i, [128, 512], BF16, at=TB + i * 6144 + 2048) for i in range(NSET)]
            qn_b = [salloc("qn%d" % i, [128, 512], BF16, at=TB + i * 6144 + 3072) for i in range(NSET)]
            qln_b = [salloc("qln%d" % i, [128, 512], F32, at=TB + i * 6144 + 4096) for i in range(NSET)]
            gst_b = qsq_b
            qst = {"n": 0}
            pending = []

            def tick(newgen=None):
                olds = list(pending)
                if newgen is not None:
                    try:
                        next(newgen)
                        pending.append(newgen)
                    except StopIteration:
                        pass
                for g_ in olds:
                    try:
                        next(g_)
                    except StopIteration:
                        pending.remove(g_)

            def qk_evac(bi, dst, dkey, gcol, gkey, h, c0, W, ti, isc):
                n_ = qst["n"]
                i = n_ % NSET
                qst["n"] += 1
                qraw, qsq, qln, qn = qraw_b[i], qsq_b[i], qln_b[i], qn_b[i]
                t1, t2 = qln, qraw
                S.op("act", lambda: A.activation(out=qraw[:, :W], in_=bank(bi, W), func=AF.Identity), [bk(bi)], [("qraw", i)])
                S.op("dve", lambda: V.tensor_tensor(out=qsq[:, :W], in0=qraw[:, :W], in1=qraw[:, :W], op=ALU.mult), [("qraw", i)], [("qsq", i)])
                yield
                b2 = nextbank(4, 8)
                S.op("pe", lambda: PE.matmul(bank(b2, W), lhsT=bdB, rhs=qsq[:, :W], start=True, stop=True), [("qsq", i), ("cb",)], [bk(b2)])
                S.op("act", lambda: A.activation(out=qln[:, :W], in_=bank(b2, W), func=AF.Ln, scale=1.0 / 64, bias=epsc[:]), [bk(b2), ("epsc",)], [("qln", i)])
                S.op("act", lambda: A.activation(out=qln[:, :W], in_=qln[:, :W], func=AF.Exp, scale=-0.5), [("qln", i)], [("qln", i)])
                if isc:
                    S.op("dve", lambda: V.scalar_tensor_tensor(out=dst[:, h, c0:c0 + W], in0=qraw[:, :W], scalar=gcol, in1=qln[:, :W], op0=ALU.mult, op1=ALU.mult),
                         [("qraw", i), ("qln", i), gkey], [(dkey, h, ti)])
                    return
                S.op("dve", lambda: V.scalar_tensor_tensor(out=qn[:, :W], in0=qraw[:, :W], scalar=gcol, in1=qln[:, :W], op0=ALU.mult, op1=ALU.mult),
                     [("qraw", i), ("qln", i), gkey], [("qn", i)])
                yield
                b3 = nextbank(4, 8)
                S.op("pe", lambda: PE.matmul(bank(b3, W), lhsT=rotB, rhs=qn[:, :W], start=True, stop=True), [("qn", i), ("cb",)], [bk(b3)])
                S.op("pool", lambda: G.tensor_tensor(out=t1[:, :W], in0=qn[:, :W], in1=cosT[:, c0:c0 + W], op=ALU.mult), [("qn", i), ("rope",)], [("qln", i)])
                S.op("dve", lambda: V.tensor_tensor(out=t2[:, :W], in0=bank(b3, W), in1=sinT[:, c0:c0 + W], op=ALU.mult), [bk(b3), ("rope",)], [("qraw", i)])
                yield
                eng, Eh = ("pool", G) if n_ % 2 == 0 else ("dve", V)
                S.op(eng, lambda: Eh.tensor_tensor(out=dst[:, h, c0:c0 + W], in0=t1[:, :W], in1=t2[:, :W], op=ALU.add), [("qln", i), ("qraw", i)], [(dkey, h, ti)])

            for g in range(2):
                wb, wk = load_wgroup(g * 512)
                for cbk in range(4):
                    h = g * 4 + cbk
                    for ti, (c0, W, isc) in enumerate(all_tiles):
                        bi = proj_block(wb, wk, cbk, c0, W, ti)
                        tick(qk_evac(bi, KT, "KT", kg[:, l:l + 1], ("kg",), h, c0, W, ti, isc))
            for g in range(2):
                wb, wk = load_wgroup(1024 + g * 512)
                for t in range(18):
                    ti = t // 4 if t < 16 else 4
                    bi = nextbank(0, 4)
                    for kc in range(8):
                        S.op("pe", lambda kc=kc, t=t, bi=bi, wb=wb: PE.matmul(bank(bi), lhsT=hT[:, kc, t * 128:(t + 1) * 128], rhs=wb[:, kc, :], start=(kc == 0), stop=(kc == 7)),
                             [wk, ("hT", kc, ti)], [bk(bi)])
                    tick()
                    copy_any(Vt[:, t, g * 512:(g + 1) * 512], bank(bi), [bk(bi)], [("Vt", t, g)])
            for g in range(2):
                wb, wk = load_wgroup(2048 + g * 512)
                for cbk in range(4):
                    h = g * 4 + cbk
                    for ti, (c0, W, isc) in enumerate(out_tiles):
                        bi = proj_block(wb, wk, cbk, c0, W, ti)
                        tick(qk_evac(bi, QT, "QT", qg[:, l:l + 1], ("qg",), h, c0, W, ti, isc))
            for gg in range(4):
                wb, wk = load_wgroup(3584 + gg * 512)
                for cbk in range(4):
                    ch = gg * 4 + cbk
                    for ti, (c0, W, isc) in enumerate(out_tiles):
                        bi = proj_block(wb, wk, cbk, c0, W, ti)
                        tick()
                        i = qst["n"] % NSET
                        qst["n"] += 1
                        gs = gst_b[i]
                        S.op("act", lambda gs=gs, bi=bi, W=W: A.activation(out=gs[:, :W], in_=bank(bi, W), func=AF.Sigmoid), [bk(bi)], [("qsq", i)])
                        S.dma("sp", lambda gs=gs, ch=ch, c0=c0, W=W: nc.sync.dma_start(out=gates[ti][:, ch * 512:ch * 512 + W], in_=gs[:, :W]), [("qsq", i)], [("gates", ch, ti)])
            while pending:
                tick()
            S.free("hT", "wb", "qraw", "qsq", "qln", "qn")

            oT = salloc("oT", [128, 8, TA], BF16, at=R0)
            NPT = 4
            PT_b = [salloc("PT%d" % i, [128, 1024], BF16, at=R4 + i * 2048) for i in range(NPT)]
            AT = R4 + NPT * 2048
            at_b = [[salloc("at%d_%d" % (i, k), [128, 512], BF16 if k == 5 else F32, at=AT + (i * 8 + k) * 2048) for k in range(8)] for i in range(2)]
            q_tiles = [(ti, c0, W, isc) for ti, (c0, W, isc) in enumerate(out_tiles)]
            items = []
            for h in range(NH):
                for (ti, c0, W, isc) in q_tiles:
                    kts = [16, 17] if isc else list(range(18))
                    for ki, kt in enumerate(kts):
                        items.append((h, ti, c0, W, isc, kt, ki == 0, ki == len(kts) - 1))

            def s_stage(n):
                h, ti, c0, W, isc, kt, first, last = items[n]
                kti = kt // 4 if kt < 16 else 4
                sa = (n % 2) * 2
                pi = n % NPT
                PT = PT_b[pi]
                S.op("pe", lambda: PE.matmul(bank(sa, W), lhsT=KT[0:64, h, kt * 128:(kt + 1) * 128], rhs=QT[0:64, h, c0:c0 + W], start=True, stop=True),
                     [("KT", h, kti), ("QT", h, ti)], [bk(sa)])
                S.op("pe", lambda: PE.matmul(bank(sa + 1, W), lhsT=KT[64:128, h, kt * 128:(kt + 1) * 128], rhs=QT[64:128, h, c0:c0 + W], start=True, stop=True),
                     [("KT", h, kti), ("QT", h, ti)], [bk(sa + 1)])
                S.op("act", lambda: A.activation(out=PT[:].rearrange("p (i w) -> p i w", i=2)[:, :, :W],
                                                 in_=ps[:, sa * 512:(sa + 2) * 512].rearrange("p (i w) -> p i w", i=2)[:, :, :W], func=AF.Exp, scale=0.125),
                     [bk(sa), bk(sa + 1)], [("PT", pi)])

            grp = {"g": -1}

            def av1_stage(n):
                h, ti, c0, W, isc, kt, first, last = items[n]
                pi = n % NPT
                PT = PT_b[pi]
                if first:
                    grp["g"] += 1
                qi = grp["g"] % 2
                acc1 = at_b[qi][4]
                S.op("pe", lambda: PE.matmul(bank(4, W), lhsT=Vt[:, kt, h * 128:(h + 1) * 128], rhs=PT[:, 0:W], start=first, stop=last),
                     [("Vt", kt, h // 4), ("PT", pi)], [bk(4)])
                if first:
                    S.op("dve", lambda: V.tensor_copy(out=acc1[:, :W], in_=PT[:, 0:W]), [("PT", pi)], [("at", qi, 4)])
                else:
                    S.op("dve", lambda: V.tensor_tensor(out=acc1[:, :W], in0=acc1[:, :W], in1=PT[:, 0:W], op=ALU.add), [("PT", pi), ("at", qi, 4)], [("at", qi, 4)])

            def av2_stage(n):
                h, ti, c0, W, isc, kt, first, last = items[n]
                pi = n % NPT
                PT = PT_b[pi]
                S.op("pe", lambda: PE.matmul(bank(6, W), lhsT=Vt[:, kt, h * 128:(h + 1) * 128], rhs=PT[:, 512:512 + W], start=first, stop=last),
                     [("Vt", kt, h // 4), ("PT", pi)], [bk(6)])
                S.op("pe", lambda: PE.matmul(bank(7, W), lhsT=onesB, rhs=PT[:, 512:512 + W], start=first, stop=last),
                     [("cb",), ("PT", pi)], [bk(7)])

            def post1(n):
                h, ti, c0, W, isc, kt, first, last = items[n]
                qi = grp["g"] % 2
                l1, l2, o1, o2, acc1, acc1b = at_b[qi][0:6]
                kk = lambda k: ("at", qi, k)
                S.op("dve", lambda: V.tensor_copy(out=o1[:, :W], in_=bank(4, W)), [bk(4)], [kk(2)])
                S.op("act", lambda: A.activation(out=l2[:, :W], in_=bank(7, W), func=AF.Ln), [bk(7)], [kk(1)])
                S.op("dve", lambda: V.tensor_scalar(out=o2[:, :W], in0=bank(6, W), scalar1=neglam[:, l:l + 1], scalar2=None, op0=ALU.mult), [bk(6), ("neglam",)], [kk(3)])
                S.op("dve", lambda: V.tensor_copy(out=acc1b[:, :W], in_=acc1[:, :W]), [kk(4)], [kk(5)])
                S.op("act", lambda: A.activation(out=l2[:, :W], in_=l2[:, :W], func=AF.Exp, scale=-1.0), [kk(1)], [kk(1)])
                return (h, ti, c0, W, qi)

            def post2(info):
                h, ti, c0, W, qi = info
                l1, l2, o1, o2, acc1, acc1b = at_b[qi][0:6]
                kk = lambda k: ("at", qi, k)
                S.op("pe", lambda: PE.matmul(bank(5, W), lhsT=onesB, rhs=acc1b[:, :W], start=True, stop=True), [kk(5), ("cb",)], [bk(5)])
                S.op("act", lambda: A.activation(out=l1[:, :W], in_=bank(5, W), func=AF.Ln), [bk(5)], [kk(0)])
                S.op("act", lambda: A.activation(out=l1[:, :W], in_=l1[:, :W], func=AF.Exp, scale=-1.0), [kk(0)], [kk(0)])
                S.op("pool", lambda: G.tensor_tensor(out=o2[:, :W], in0=o2[:, :W], in1=l2[:, :W], op=ALU.mult), [kk(3), kk(1)], [kk(3)])
                S.op("pool", lambda: G.tensor_tensor(out=o1[:, :W], in0=o1[:, :W], in1=l1[:, :W], op=ALU.mult), [kk(2), kk(0)], [kk(2)])
                S.op("pool", lambda: G.tensor_tensor(out=oT[:, h, c0:c0 + W], in0=o2[:, :W], in1=o1[:, :W], op=ALU.add), [kk(2), kk(3)], [("oT", h, ti)])

            wpa = salloc("wpa", [128, 8, D], BF16, at=R1)
            wpf = salloc("wpf", [128, 4, D], BF16, at=R1 + 18432)
            wo = salloc("wo", [128, 8, D], BF16, at=R2)
            wpa_v = wpa_d[l].rearrange("(kc p) c -> p kc c", p=128)
            wpf_v = wpf_d[l].rearrange("(kc p) c -> p kc c", p=128)
            wo_v = wo_d[l].rearrange("(kc p) c -> p kc c", p=128)
            last_ti = q_tiles[-1][0]

            def dead(nm, heads):
                return [(nm, h_, t_) for h_ in heads for t_ in range(5)]

            s_stage(0)
            s_stage(1)
            pend2 = []
            for n in range(len(items)):
                av1_stage(n)
                if n + 2 < len(items):
                    s_stage(n + 2)
                av2_stage(n)
                while pend2:
                    post2(pend2.pop(0))
                if items[n][7]:
                    pend2.append(post1(n))
                    if items[n][1] == last_ti and items[n][0] == 3:
                        for hf in range(2):
                            S.dma("pool", lambda hf=hf: nc.gpsimd.dma_start(out=wpa[:, :, hf * 512:(hf + 1) * 512], in_=wpa_v[:, :, hf * 512:(hf + 1) * 512]), [], [("wpa", hf)] + dead("KT", range(4)))
                        for hf in range(2):
                            S.dma("pool", lambda hf=hf: nc.gpsimd.dma_start(out=wo[:, :, hf * 512:(hf + 1) * 512], in_=wo_v[:, :, hf * 512:(hf + 1) * 512]), [], [("wo", hf)] + dead("QT", range(4)))
                    if items[n][1] == last_ti and items[n][0] == 5:
                        for hf in range(2):
                            S.dma("pool", lambda hf=hf: nc.gpsimd.dma_start(out=wpf[:, :, hf * 512:(hf + 1) * 512], in_=wpf_v[:, :, hf * 512:(hf + 1) * 512]), [], [("wpf", hf)] + dead("KT", [4, 5]))
            while pend2:
                post2(pend2.pop(0))
            S.free("at", "PT", "KT", "QT", "Vt")
            M0 = R1 + 53248
            gat_b = [salloc("gat%d" % i, [128, 16, 512], BF16, at=M0 + i * 16384) for i in range(2)]
            mx_b = [salloc("mx%d" % i, [128, 8, 512], F32, at=M0 + 32768 + i * 16384) for i in range(2)]
            fot_b = [salloc("fot%d" % i, [128, 4, 512], BF16, at=M0 + 65536 + i * 4096) for i in range(2)]
            uT_b = [salloc("uT%d" % i, [128, 8, 512], BF16, at=M0 + 73728 + i * 8192) for i in range(2)]
            u_b = [salloc("u%d" % i, [128, 512], BF16, at=M0 + 90112 + i * 1024) for i in range(4)]

            def p5_load(tix):
                c0, W, isc = out_tiles[tix]
                i = tix % 2
                gat, mx, fot = gat_b[i], mx_b[i], fot_b[i]
                S.dma("sp", lambda: nc.sync.dma_start(out=gat[:, :, :W], in_=gates[tix].rearrange("p (c t) -> p c t", c=16)[:, :, :W]),
                      [("gates", ch, tix) for ch in range(16)], [("gat", i)])
                S.dma("sp", lambda: nc.sync.dma_start(out=fot[:, :, :W], in_=fos[tix].rearrange("p (j t) -> p j t", j=4)[:, :, :W]), [("fos", tix)], [("fot", i)])
                S.dma("sp", lambda: nc.sync.dma_start(out=mx[:, :, :W], in_=xs_cols(c0, W)), [("xs", tix)], [("mx", i)])

            p5_load(0)
            SL0 = R1 + 124928
            osq_b = [salloc("osq%d" % i, [128, 4, 512], BF16, at=SL0 + i * 13312) for i in range(2)]
            oln_b = [salloc("oln%d" % i, [128, 4, 512], F32, at=SL0 + i * 13312 + 4096) for i in range(2)]
            units = [[(h, ti, c0, W) for (ti, c0, W, isc) in q_tiles if not isc] for h in range(NH)]
            if ctx_out:
                units += [[(h, 4, T, TC) for h in range(4)], [(h, 4, T, TC) for h in range(4, 8)]]
            def sl_a(ui):
                unit = units[ui]
                si = ui % 2
                osq, oln = osq_b[si], oln_b[si]
                ne = len(unit)
                Wm = unit[0][3]
                for e, (h, ti, c0, W) in enumerate(unit):
                    eng, Eh = ("pool", G) if e % 2 == 0 else ("dve", V)
                    S.op(eng, lambda e=e, h=h, c0=c0, W=W, Eh=Eh: Eh.tensor_tensor(out=osq[:, e, :W], in0=oT[:, h, c0:c0 + W], in1=oT[:, h, c0:c0 + W], op=ALU.mult), [("oT", h, ti)], [("osq", si, e)])
                    S.op("pe", lambda e=e, W=W: PE.matmul(bank(si * 4 + e, W), lhsT=onesB, rhs=osq[:, e, :W], start=True, stop=True), [("osq", si, e), ("cb",)], [bk(si * 4 + e)])
                S.op("act", lambda: A.activation(out=oln[:, :ne, :Wm], in_=ps[:, si * 2048:(si + 1) * 2048].rearrange("p (e w) -> p e w", e=4)[:, :ne, :Wm], func=AF.Ln, scale=1.0 / 128, bias=epsc[:]),
                     [bk(si * 4 + e) for e in range(ne)] + [("epsc",)], [("oln", si)])
                S.op("act", lambda: A.activation(out=oln[:, :ne, :Wm], in_=oln[:, :ne, :Wm], func=AF.Exp, scale=-0.5), [("oln", si)], [("oln", si)])

            def sl_b(ui):
                unit = units[ui]
                si = ui % 2
                oln = oln_b[si]
                for e, (h, ti, c0, W) in enumerate(unit):
                    S.op("dve", lambda e=e, h=h, c0=c0, W=W: V.scalar_tensor_tensor(out=oT[:, h, c0:c0 + W], in0=oT[:, h, c0:c0 + W], scalar=gsub[:, l:l + 1], in1=oln[:, e, :W], op0=ALU.mult, op1=ALU.mult),
                         [("oT", h, ti), ("oln", si), ("gsub",)], [("oT", h, ti)])

            sl_a(0)
            for ui in range(len(units)):
                if ui + 1 < len(units):
                    sl_a(ui + 1)
                sl_b(ui)
            S.free("osq", "oln")

            F0 = R1 + NJ * TA * 2
            wgu_b = [salloc("wgu0", [128, 8, 512], BF16, at=R1 + 147456), salloc("wgu1", [128, 8, 512], BF16, at=F0)]
            wd_b = [salloc("wdn%d" % i, [128, NJ, 256], BF16, at=F0 + 8192 + i * 11264) for i in range(2)]
            sg_b = [salloc("sgl%d" % i, [128, 512], F32, at=F0 + 30720 + i * 2048) for i in range(2)]
            fx_b = [salloc("fx%d" % i, [128, 512], F32, at=F0 + 34816 + i * 2048) for i in range(3)]
            assert F0 + 34816 + 3 * 2048 <= R1 + 147456 and R1 + 147456 + 8192 <= SB_END
            wgu_v = wgu_d[l].rearrange("(kc p) c -> p kc c", p=128)
            wd_v = wd_d[l].rearrange("(j p) c -> p j c", p=128)

            def load_wgu(jj):
                wi = jj % 2
                wg = wgu_b[wi]
                S.dma("pool", lambda: nc.gpsimd.dma_start(out=wg[:, :, 0:256], in_=wgu_v[:, :, jj * 256:(jj + 1) * 256]), [], [("wgu", wi, 0)])
                S.dma("pool", lambda: nc.gpsimd.dma_start(out=wg[:, :, 256:512], in_=wgu_v[:, :, DFF + jj * 256:DFF + (jj + 1) * 256]), [], [("wgu", wi, 1)])

            def load_wd(mp):
                wi = mp % 2
                wdn = wd_b[wi]
                S.dma("pool", lambda: nc.gpsimd.dma_start(out=wdn[:, 0:11, :], in_=wd_v[:, 0:11, mp * 256:(mp + 1) * 256]), [], [("wdn", wi, 0)])
                S.dma("pool", lambda: nc.gpsimd.dma_start(out=wdn[:, 11:22, :], in_=wd_v[:, 11:22, mp * 256:(mp + 1) * 256]), [], [("wdn", wi, 1)])

            load_wgu(0)
            h2T = salloc("h2T", [128, 8, TA], BF16, at=R0)
            nsq2 = salloc("nsq2", [128, 8, 512], BF16, at=R1 + 26624)
            nln2 = salloc("nln2", [128, 512], F32, at=R1 + 26624 + 8192)

            ust = {"n": 0}
            for tix, (c0, W, isc) in enumerate(out_tiles):
                who = 2 if isc else b
                i = tix % 2
                gat, mx, uT, fot = gat_b[i], mx_b[i], uT_b[i], fot_b[i]
                if tix + 1 < len(out_tiles):
                    p5_load(tix + 1)
                for mc in range(8):
                    ba = nextbank(0, 8)
                    for kc in range(8):
                        S.op("pe", lambda kc=kc, mc=mc, ba=ba: PE.matmul(bank(ba, W), lhsT=wpa[:, kc, mc * 128:(mc + 1) * 128], rhs=oT[:, kc, c0:c0 + W], start=(kc == 0), stop=(kc == 7)),
                             [("wpa", mc // 4), ("oT", kc, tix)], [bk(ba)])
                    bf = nextbank(0, 8)
                    for kc in range(4):
                        S.op("pe", lambda kc=kc, mc=mc, bf=bf: PE.matmul(bank(bf, W), lhsT=wpf[:, kc, mc * 128:(mc + 1) * 128], rhs=fot[:, kc, :W], start=(kc == 0), stop=(kc == 3)),
                             [("wpf", mc // 4), ("fot", i)], [bk(bf)])
                    ui = ust["n"] % 2
                    ust["n"] += 1
                    u1, u2 = u_b[2 * ui], u_b[2 * ui + 1]
                    S.op("dve", lambda u1=u1, ba=ba, mc=mc: V.tensor_tensor(out=u1[:, :W], in0=bank(ba, W), in1=gat[:, mc, :W], op=ALU.mult), [bk(ba), ("gat", i)], [("u", 2 * ui)])
                    S.op("dve", lambda u2=u2, bf=bf, mc=mc: V.tensor_tensor(out=u2[:, :W], in0=bank(bf, W), in1=gat[:, 8 + mc, :W], op=ALU.mult), [bk(bf), ("gat", i)], [("u", 2 * ui + 1)])
                    S.op("pool", lambda u1=u1, u2=u2, mc=mc: G.tensor_tensor(out=uT[:, mc, :W], in0=u1[:, :W], in1=u2[:, :W], op=ALU.add), [("u", 2 * ui), ("u", 2 * ui + 1)], [("uT", i, mc)])
                for mc in range(8):
                    bz = nextbank(0, 8)
                    for kc in range(8):
                        S.op("pe", lambda kc=kc, mc=mc, bz=bz: PE.matmul(bank(bz, W), lhsT=wo[:, kc, mc * 128:(mc + 1) * 128], rhs=uT[:, kc, :W], start=(kc == 0), stop=(kc == 7)),
                             [("wo", mc // 4), ("uT", i, kc)], [bk(bz)])
                    S.op("dve", lambda mc=mc, bz=bz: V.scalar_tensor_tensor(out=mx[:, mc, :W], in0=bank(bz, W), scalar=mcol(l, 2, mc, who), in1=mx[:, mc, :W], op0=ALU.mult, op1=ALU.add),
                         [bk(bz), ("mx", i), ("modT",)], [("mx", i)])
                S.dma("sp", lambda mx=mx, c0=c0, W=W: nc.sync.dma_start(out=xs_cols(c0, W), in_=mx[:, :, :W]), [("mx", i)], [("xs", tix)])
                S.op("pool", lambda mx=mx, W=W: G.tensor_tensor(out=nsq2[:, 0:4, :W], in0=mx[:, 0:4, :W], in1=mx[:, 0:4, :W], op=ALU.mult), [("mx", i)], [("nsq2", 0)])
                S.op("dve", lambda mx=mx, W=W: V.tensor_tensor(out=nsq2[:, 4:8, :W], in0=mx[:, 4:8, :W], in1=mx[:, 4:8, :W], op=ALU.mult), [("mx", i)], [("nsq2", 1)])
                bn = nextbank(0, 8)
                for c in range(8):
                    S.op("pe", lambda c=c, bn=bn, W=W: PE.matmul(bank(bn, W), lhsT=onesB, rhs=nsq2[:, c, :W], start=(c == 0), stop=(c == 7)), [("nsq2", c // 4), ("cb",)], [bk(bn)])
                S.op("act", lambda bn=bn, W=W: A.activation(out=nln2[:, :W], in_=bank(bn, W), func=AF.Ln, scale=1.0 / D, bias=epsc[:]), [bk(bn), ("epsc",)], [("nln2",)])
                S.op("act", lambda W=W: A.activation(out=nln2[:, :W], in_=nln2[:, :W], func=AF.Exp, scale=-0.5), [("nln2",)], [("nln2",)])
                S.op("dve", lambda mx=mx, W=W: V.tensor_tensor(out=mx[:, :, :W], in0=mx[:, :, :W], in1=nln2[:, :W].unsqueeze(1).broadcast_to([128, 8, W]), op=ALU.mult), [("mx", i), ("nln2",)], [("mx", i)])
                for c in range(8):
                    S.op("act", lambda mx=mx, c=c, c0=c0, W=W, who=who: A.activation(out=h2T[:, c, c0:c0 + W], in_=mx[:, c, :W], func=AF.Identity,
                                                                                     scale=Gm[:, l, who, 1, c:c + 1], bias=mcol(l, 3, c, who)),
                         [("mx", i), ("Gm",), ("modT",)], [("hT", c, tix), ("oT", c, tix)])
            hT = h2T
            S.free("oT", "wpa", "wpf", "wo", "gat", "mx", "uT", "fot", "u", "nsq2", "nln2")

            actT = salloc("actT", [128, NJ, TA], BF16, at=R1)
            fst = {"s": 0, "x": 0}
            for jj in range(NJ // 2):
                wi = jj % 2
                wg = wgu_b[wi]
                if jj + 1 < NJ // 2:
                    load_wgu(jj + 1)
                if jj == 6:
                    load_wd(0)
                if jj == 9:
                    load_wd(1)
                for jl in range(2):
                    j = jj * 2 + jl
                    for ti, (c0, W, isc) in enumerate(out_tiles):
                        bg = nextbank(0, 8)
                        for kc in range(8):
                            S.op("pe", lambda kc=kc, jl=jl, bg=bg, wg=wg, c0=c0, W=W: PE.matmul(bank(bg, W), lhsT=wg[:, kc, jl * 128:(jl + 1) * 128], rhs=hT[:, kc, c0:c0 + W], start=(kc == 0), stop=(kc == 7)),
                                 [("wgu", wi, 0), ("hT", kc, ti)], [bk(bg)])
                        bu = nextbank(0, 8)
                        for kc in range(8):
                            S.op("pe", lambda kc=kc, jl=jl, bu=bu, wg=wg, c0=c0, W=W: PE.matmul(bank(bu, W), lhsT=wg[:, kc, 256 + jl * 128:256 + (jl + 1) * 128], rhs=hT[:, kc, c0:c0 + W], start=(kc == 0), stop=(kc == 7)),
                                 [("wgu", wi, 1), ("hT", kc, ti)], [bk(bu)])
                        si = fst["s"] % 2
                        fst["s"] += 1
                        sgl = sg_b[si]
                        S.op("act", lambda sgl=sgl, bg=bg, W=W: A.activation(out=sgl[:, :W], in_=bank(bg, W), func=AF.Silu), [bk(bg)], [("sgl", si)])
                        S.op("dve", lambda sgl=sgl, bu=bu, j=j, c0=c0, W=W: V.tensor_tensor(out=actT[:, j, c0:c0 + W], in0=bank(bu, W), in1=sgl[:, :W], op=ALU.mult), [bk(bu), ("sgl", si)], [("actT", j, ti)])
            for mp in range(4):
                wi = mp % 2
                wdn = wd_b[wi]
                if mp >= 1 and mp + 1 < 4:
                    load_wd(mp + 1)
                for ml in range(2):
                    mc = mp * 2 + ml
                    for ti, (c0, W, isc) in enumerate(out_tiles):
                        who = 2 if isc else b
                        xi = fst["x"] % 3
                        fst["x"] += 1
                        fx = fx_b[xi]
                        S.dma("sp", lambda fx=fx, mc=mc, c0=c0, W=W: nc.sync.dma_start(out=fx[:, :W], in_=xs_chunk(mc, ti, W)), [("xs", ti)], [("fx", xi)])
                        bz = nextbank(0, 8)
                        for j in range(NJ):
                            S.op("pe", lambda j=j, ml=ml, bz=bz, wdn=wdn, c0=c0, W=W: PE.matmul(bank(bz, W), lhsT=wdn[:, j, ml * 128:(ml + 1) * 128], rhs=actT[:, j, c0:c0 + W], start=(j == 0), stop=(j == NJ - 1)),
                                 [("wdn", wi, j // 11), ("actT", j, ti)], [bk(bz)])
                        S.op("dve", lambda fx=fx, bz=bz, mc=mc, W=W, who=who: V.scalar_tensor_tensor(out=fx[:, :W], in0=bank(bz, W), scalar=mcol(l, 5, mc, who), in1=fx[:, :W], op0=ALU.mult, op1=ALU.add),
                             [bk(bz), ("fx", xi), ("modT",)], [("fx", xi)])
                        S.dma("sp", lambda fx=fx, mc=mc, c0=c0, W=W: nc.sync.dma_start(out=xs_chunk(mc, ti, W), in_=fx[:, :W]), [("fx", xi)], [("xs", ti)])
            S.free("hT", "actT", "wgu", "wdn", "sgl", "fx")

        xtt_b = [salloc("oxt%d" % i, [128, 8, 512], F32, at=R0 + i * 16384) for i in range(2)]
        xo_b = [salloc("oxo%d" % i, [128, 4, D], F32, at=R0 + 32768 + i * 16384) for i in range(2)]

        def p8_load(g):
            xtt = xtt_b[g % 2]
            S.dma("sp", lambda: nc.sync.dma_start(out=xtt[:], in_=xs_cols(g * 512, 512)), [("xs", g)], [("oxt", g % 2)])

        p8_load(0)
        p8_load(1)
        for g in range(4):
            xtt = xtt_b[g % 2]
            xo = xo_b[g % 2]
            for t in range(4):
                pb = 2 * t
                for c in range(8):
                    S.op("pe", lambda t=t, c=c, pb=pb: PE.transpose(out=ps[:, pb * 512 + c * 128:pb * 512 + (c + 1) * 128], in_=xtt[:, c, t * 128:(t + 1) * 128], identity=identF[:]),
                         [("oxt", g % 2), ("identF",)], [bk(pb + c // 4)])
                copy_any(xo[:, t, :], ps[:, pb * 512:pb * 512 + 1024], [bk(pb), bk(pb + 1)], [("oxo", g % 2, t)])
            if g + 2 < 4:
                p8_load(g + 2)
            S.dma("sp", lambda: nc.sync.dma_start(out=out_d[b, g * 512:(g + 1) * 512, :].rearrange("(t p) d -> p t d", p=128), in_=xo[:]),
                  [("oxo", g % 2, t) for t in range(4)], [("out", b, g)])
        S.free("oxt", "oxo")

    S.finish()
    return nc, S


_CONST = {}


def _consts():
    if _CONST:
        return _CONST
    bf = ml_dtypes.bfloat16
    ones = np.ones((128, 128), np.float32)
    bd = np.zeros((128, 128), np.float32)
    bd[:64, :64] = 1.0
    bd[64:, 64:] = 1.0
    rot = np.zeros((128, 128), np.float32)
    for m in range(128):
        d = m % 64
        half = (d // 16) % 2
        if half == 0:
            rot[m + 16, m] = -1.0
        else:
            rot[m - 16, m] = 1.0
    cidx = np.arange(128)
    angc = 2.0 * np.pi * ((cidx[:, None] * cidx[None, :]) % 128) / 128.0
    Cc = np.cos(angc)
    Scn = -np.sin(angc)
    _CONST["cbf"] = np.ascontiguousarray(np.stack([ones, bd, rot, Cc, Scn], axis=1)).astype(bf)
    _CONST["ident"] = np.eye(128, dtype=np.float32)
    t = np.arange(T)
    r = (t // 64).astype(np.float32)
    col = (t % 64).astype(np.float32)
    inv_freq = (10000.0 ** (-np.arange(0, 32, 2, dtype=np.float32) / 32.0)).astype(np.float32)
    ang_r = r[:, None] * inv_freq[None, :]
    ang_c = col[:, None] * inv_freq[None, :]
    ang = np.concatenate([ang_r, ang_r, ang_c, ang_c], axis=-1)
    cosT = np.cos(ang).T
    sinT = np.sin(ang).T
    rope = np.stack([np.concatenate([cosT, cosT], 0), np.concatenate([sinT, sinT], 0)], axis=1)
    _CONST["rope"] = np.ascontiguousarray(rope).astype(bf)
    tt = np.arange(T, dtype=np.int64)
    angt = 2.0 * np.pi * ((tt[:, None] * tt[None, :]) % T).astype(np.float64) / T
    sc = 1.0 / math.sqrt(T * 128.0)
    def _lay(tab):
        a = tab.astype(np.float32).reshape(2, 8, 128, 4, 512).transpose(3, 0, 2, 1, 4)
        return np.ascontiguousarray(a).reshape(4, 2, 128, 8 * 512).astype(bf)
    _CONST["ctab"] = _lay(np.cos(angt) * sc)
    _CONST["stab"] = _lay(np.sin(angt) * sc)
    t2 = np.arange(TC, dtype=np.int64)
    ang2 = 2.0 * np.pi * ((t2[:, None] * t2[None, :]) % TC).astype(np.float64) / TC
    sc2 = 1.0 / math.sqrt(TC * 128.0)
    c2 = (np.cos(ang2) * sc2).reshape(2, 128, TC).transpose(1, 0, 2)
    s2 = (np.sin(ang2) * sc2).reshape(2, 128, TC).transpose(1, 0, 2)
    _CONST["c256"] = np.ascontiguousarray(np.stack([c2, s2], axis=1)).astype(np.float32).astype(bf)
    return _CONST


def _cols(v):
    v = np.asarray(v, np.float32)
    n = v.shape[-1] // 128
    return np.ascontiguousarray(np.moveaxis(v.reshape(v.shape[:-1] + (n, 128)), -1, 0))


_PROG = {}


def _program():
    if "nc" not in _PROG:
        _, S1 = build_program(None)
        rank = {e: {v: i + 1 for i, v in enumerate(sorted(S1.needed[e]))} for e in Sched.ENGS}
        nc, _ = build_program(rank)
        _PROG["nc"] = nc
    return _PROG["nc"]


def make_in_maps(inp, ncores=8):
    C = _consts()
    f = lambda a: np.ascontiguousarray(np.asarray(a, np.float32))
    shared = {
        "w_ada": f(inp["w_ada"]), "w_in": f(inp["w_in"]), "w_proj_attn": f(inp["w_proj_attn"]),
        "w_proj_fourier": f(inp["w_proj_fourier"]), "w_out": f(inp["w_out"]),
        "w_gate_up": f(inp["w_gate_up"]), "w_down": f(inp["w_down"]),
        "bada": _cols(inp["b_ada"]), "n1g": _cols(inp["norm1_g"]), "n2g": _cols(inp["norm2_g"]),
        "qg": np.ascontiguousarray(np.tile(f(inp["q_norm_g"]), (1, 2)).T),
        "kg": np.ascontiguousarray(np.tile(f(inp["k_norm_g"]), (1, 2)).T),
        "sublng": np.ascontiguousarray(f(inp["subln_g"]).T),
        "lamv": np.ascontiguousarray(np.broadcast_to(
            np.stack([f(inp["lambda_q1"]), f(inp["lambda_k1"]), f(inp["lambda_q2"]), f(inp["lambda_k2"])], axis=1)[None], (128, DEPTH, 4, 64))),
        "cbf": C["cbf"], "ident": C["ident"], "rope": C["rope"], "c256": C["c256"], "ctab": C["ctab"], "stab": C["stab"],
    }
    x = f(inp["x"])
    ctx = f(inp["ctx"])
    c = f(inp["c"])
    cc = f(inp["c_ctx"])
    maps = []
    for k in range(ncores):
        cv = np.stack([c[2 * k], c[2 * k + 1], cc, cc], axis=0)
        m = dict(shared)
        m["x2"] = x[2 * k:2 * k + 2]
        m["ctx2"] = ctx[2 * k:2 * k + 2]
        m["cT"] = np.ascontiguousarray(cv.reshape(4, 8, 128).transpose(2, 1, 0))
        maps.append(m)
    return maps


def kernel(**inputs):
    nc = _program()
    maps = make_in_maps(inputs, 8)
    res = run_bass_kernel_spmd(nc, maps, core_ids=list(range(8)))
    return np.concatenate([np.asarray(r["out"], np.float32) for r in res.results], axis=0)
```
